# Optimizing a Trainium2 kernel written in Bass

```python
import math
import jax, jax.numpy as jnp
from jax import lax
import numpy as np

D_MODEL = 1024
BATCH = 4
SEQ = 8192
DEPTH = 4

CTX_LEN = 256
GRID_W = 64
N_ADA = 9
D_FF = 2816
EPS = 1e-6
ROPE_THETA = 10000.0
Q_BLOCK = 128

MLA_HEADS = 8
MLA_NOPE = 64
MLA_ROPE = 32
MLA_V = 64
MLA_Q_LORA = 384
MLA_KV_LORA = 256
MLA_SCALE = (MLA_NOPE + MLA_ROPE) ** -0.5

DIFF_HEADS = 8
DIFF_HD = 32
DIFF_V = 2 * DIFF_HD
DIFF_SCALE = DIFF_HD ** -0.5

POOL_WINDOWS = (2, 4, 8, 16)
POOL_GROUPS = 4
POOL_W = 512
POOL_G = POOL_W // POOL_GROUPS

N_BRANCH = 3
MLA_OUT_W = MLA_HEADS * MLA_V
DIFF_OUT_W = DIFF_HEADS * DIFF_V

IN_WIDTHS = (MLA_Q_LORA, MLA_KV_LORA, MLA_ROPE,
             DIFF_HEADS * 2 * DIFF_HD, DIFF_HEADS * 2 * DIFF_HD, DIFF_HEADS * DIFF_V,
             POOL_W, N_BRANCH * D_MODEL)
IN_COLS = sum(IN_WIDTHS)

kernel_name = "hybrid_mla_diffattn_pool_macaron_dit"

KEY_NAMES = ("mla_k", "mla_v", "dk1", "dk2", "dv")


def rmsnorm(x, g):
    x32 = x.astype(jnp.float32)
    y = x32 * lax.rsqrt(jnp.mean(x32 * x32, axis=-1, keepdims=True) + EPS)
    return (y * g.astype(jnp.float32)).astype(x.dtype)


def axial_rope_tables(n, dim):
    rows = n // GRID_W
    row_ids = jnp.repeat(jnp.arange(rows), GRID_W).astype(jnp.float32)
    col_ids = jnp.tile(jnp.arange(GRID_W), rows).astype(jnp.float32)
    n_freq = dim // 4
    inv_freq = ROPE_THETA ** (-jnp.arange(n_freq, dtype=jnp.float32) / n_freq)
    ang = jnp.concatenate([row_ids[:, None] * inv_freq, col_ids[:, None] * inv_freq], axis=-1)
    return jnp.cos(ang), jnp.sin(ang)


def apply_rope(x, tables):
    cos, sin = tables
    half = x.shape[-1] // 2
    x32 = x.astype(jnp.float32)
    x1, x2 = x32[..., :half], x32[..., half:]
    c, s = cos[:, None, :], sin[:, None, :]
    return jnp.concatenate([x1 * c - x2 * s, x1 * s + x2 * c], axis=-1).astype(x.dtype)


def swiglu(u, w_gate, w_up, w_down):
    return (jax.nn.silu(u @ w_gate) * (u @ w_up)) @ w_down


def over_query_blocks(fn, *qs):
    b, n = qs[0].shape[:2]
    nb = n // Q_BLOCK
    blocks = tuple(jnp.moveaxis(q.reshape(b, nb, Q_BLOCK, *q.shape[2:]), 1, 0) for q in qs)
    out = lax.map(lambda qb: fn(*qb), blocks)
    return jnp.moveaxis(out, 0, 1).reshape(b, n, *out.shape[3:])


def softmax_attention(q, k, v, scale):
    def block(qb):
        s = jnp.einsum('bqhd,bkhd->bhqk', qb, k, preferred_element_type=jnp.float32) * scale
        p = jax.nn.softmax(s, axis=-1).astype(v.dtype)
        return jnp.einsum('bhqk,bkhd->bqhd', p, v)
    return over_query_blocks(block, q)


def differential_attention(q1, q2, k1, k2, v, lam, scale):
    def block(q1b, q2b):
        s1 = jnp.einsum('bqhd,bkhd->bhqk', q1b, k1, preferred_element_type=jnp.float32) * scale
        s2 = jnp.einsum('bqhd,bkhd->bhqk', q2b, k2, preferred_element_type=jnp.float32) * scale
        p = (jax.nn.softmax(s1, axis=-1) - lam * jax.nn.softmax(s2, axis=-1)).astype(v.dtype)
        return jnp.einsum('bhqk,bkhd->bqhd', p, v)
    return over_query_blocks(block, q1, q2)


def multiscale_pool(p):
    n = p.shape[1]
    t = jnp.arange(n)
    p32 = p.astype(jnp.float32)
    outs = []
    for g, w in enumerate(POOL_WINDOWS):
        xg = p32[..., g * POOL_G:(g + 1) * POOL_G]
        cs = jnp.concatenate([jnp.zeros_like(xg[:, :1]), lax.cumsum(xg, axis=1)], axis=1)
        lo = jnp.clip(t - w // 2, 0, n)
        hi = jnp.clip(t + w - w // 2, 0, n)
        mean = (cs[:, hi] - cs[:, lo]) / (hi - lo).astype(jnp.float32)[None, :, None]
        outs.append(mean - xg)
    return jnp.stack(outs, axis=2).astype(p.dtype)


def mixer_streams(u, w_in_l, q_norm_g, kv_norm_g, w_uq, w_ukv, rope_mla, rope_diff):
    b, n, _ = u.shape
    offs = np.cumsum(IN_WIDTHS)[:-1].tolist()
    cq, ckv, kr, dq, dk, dv, pool_in, gates = jnp.split(u @ w_in_l, offs, axis=-1)
    q = (rmsnorm(cq, q_norm_g) @ w_uq).reshape(b, n, MLA_HEADS, MLA_NOPE + MLA_ROPE)
    kv = (rmsnorm(ckv, kv_norm_g) @ w_ukv).reshape(b, n, MLA_HEADS, MLA_NOPE + MLA_V)
    q_nope, q_rope = q[..., :MLA_NOPE], q[..., MLA_NOPE:]
    k_nope, v_mla = kv[..., :MLA_NOPE], kv[..., MLA_NOPE:]
    k_rope = kr[:, :, None, :]
    dq = dq.reshape(b, n, DIFF_HEADS, 2, DIFF_HD)
    dk = dk.reshape(b, n, DIFF_HEADS, 2, DIFF_HD)
    dq1, dq2, dk1, dk2 = dq[..., 0, :], dq[..., 1, :], dk[..., 0, :], dk[..., 1, :]
    if rope_mla is not None:
        q_rope, k_rope = apply_rope(q_rope, rope_mla), apply_rope(k_rope, rope_mla)
        dq1, dq2 = apply_rope(dq1, rope_diff), apply_rope(dq2, rope_diff)
        dk1, dk2 = apply_rope(dk1, rope_diff), apply_rope(dk2, rope_diff)
    mla_q = jnp.concatenate([q_nope, q_rope], axis=-1)
    mla_k = jnp.concatenate([k_nope, jnp.broadcast_to(k_rope, (b, n, MLA_HEADS, MLA_ROPE))], axis=-1)
    return {"mla_q": mla_q, "mla_k": mla_k, "mla_v": v_mla,
            "dq1": dq1, "dq2": dq2, "dk1": dk1, "dk2": dk2,
            "dv": dv.reshape(b, n, DIFF_HEADS, DIFF_V),
            "pool_in": pool_in, "gates": gates}


def mixer_output(s, keys, lam, lam_init, subln_g, pool_proj, pool_b, pool_scale,
                 b_gate, w_br_mla, w_br_diff, w_br_pool, w_out):
    b, n, _ = s["pool_in"].shape
    o_mla = softmax_attention(s["mla_q"], keys["mla_k"], keys["mla_v"], MLA_SCALE).reshape(b, n, MLA_OUT_W)
    o_diff = differential_attention(s["dq1"], s["dq2"], keys["dk1"], keys["dk2"], keys["dv"], lam, DIFF_SCALE)
    o_diff = (rmsnorm(o_diff, subln_g) * (1.0 - lam_init)).reshape(b, n, DIFF_OUT_W)
    pooled = multiscale_pool(s["pool_in"])
    o_pool = (jnp.einsum('bngc,gcd->bngd', pooled, pool_proj) + pool_b).reshape(b, n, POOL_W) * pool_scale
    g = jax.nn.sigmoid((s["gates"].reshape(b, n, N_BRANCH, D_MODEL) + b_gate).astype(jnp.float32))
    g = g.astype(o_pool.dtype)
    merged = (g[:, :, 0] * (o_mla @ w_br_mla) + g[:, :, 1] * (o_diff @ w_br_diff)
              + g[:, :, 2] * (o_pool @ w_br_pool))
    return merged @ w_out


def setup_inputs(seed: int = 0) -> dict:
    key = jax.random.key(seed)
    ks = jax.random.split(key, 32)
    f32 = jnp.float32
    L, D, F = DEPTH, D_MODEL, D_FF

    def nrm(k, shape, scale):
        return jax.random.normal(k, shape, f32) * scale

    def gain(k, shape):
        return 1.0 + 0.02 * jax.random.normal(k, shape, f32)

    return {
        "x": nrm(ks[0], (BATCH, SEQ, D), 1.0),
        "c": nrm(ks[1], (BATCH, D), 1.0),
        "ctx": nrm(ks[2], (BATCH, CTX_LEN, D), 1.0),
        "c_ctx": nrm(ks[3], (D,), 1.0),
        "ada_w": nrm(ks[4], (L, D, N_ADA * D), 0.5 * D ** -0.5),
        "ada_b": nrm(ks[5], (L, N_ADA * D), 0.01),
        "norm_g": gain(ks[6], (L, 3, D)),
        "ffa_w_gate": nrm(ks[7], (L, D, F), D ** -0.5),
        "ffa_w_up": nrm(ks[8], (L, D, F), D ** -0.5),
        "ffa_w_down": nrm(ks[9], (L, F, D), F ** -0.5),
        "ffb_w_gate": nrm(ks[10], (L, D, F), D ** -0.5),
        "ffb_w_up": nrm(ks[11], (L, D, F), D ** -0.5),
        "ffb_w_down": nrm(ks[12], (L, F, D), F ** -0.5),
        "w_in": nrm(ks[13], (L, D, IN_COLS), D ** -0.5),
        "b_gate": nrm(ks[14], (L, N_BRANCH, D), 0.01),
        "mla_q_norm_g": gain(ks[15], (L, MLA_Q_LORA)),
        "mla_kv_norm_g": gain(ks[16], (L, MLA_KV_LORA)),
        "mla_w_uq": nrm(ks[17], (L, MLA_Q_LORA, MLA_HEADS * (MLA_NOPE + MLA_ROPE)), MLA_Q_LORA ** -0.5),
        "mla_w_ukv": nrm(ks[18], (L, MLA_KV_LORA, MLA_HEADS * (MLA_NOPE + MLA_V)), MLA_KV_LORA ** -0.5),
        "diff_lambda": nrm(ks[19], (L, 4, DIFF_HD), 0.1),
        "diff_subln_g": gain(ks[20], (L, DIFF_V)),
        "pool_proj": nrm(ks[21], (L, POOL_GROUPS, POOL_G, POOL_G), POOL_G ** -0.5),
        "pool_b": nrm(ks[22], (L, POOL_GROUPS, POOL_G), 0.01),
        "pool_scale": gain(ks[23], (L, POOL_W)),
        "w_br_mla": nrm(ks[24], (L, MLA_OUT_W, D), MLA_OUT_W ** -0.5),
        "w_br_diff": nrm(ks[25], (L, DIFF_OUT_W, D), DIFF_OUT_W ** -0.5),
        "w_br_pool": nrm(ks[26], (L, POOL_W, D), POOL_W ** -0.5),
        "w_out": nrm(ks[27], (L, D, D), D ** -0.5),
        "final_g": gain(ks[28], (D,)),
    }


def reference(x, c, ctx, c_ctx, ada_w, ada_b, norm_g,
              ffa_w_gate, ffa_w_up, ffa_w_down, ffb_w_gate, ffb_w_up, ffb_w_down,
              w_in, b_gate, mla_q_norm_g, mla_kv_norm_g, mla_w_uq, mla_w_ukv,
              diff_lambda, diff_subln_g, pool_proj, pool_b, pool_scale,
              w_br_mla, w_br_diff, w_br_pool, w_out, final_g):
    b, n, d = x.shape
    rope_mla = axial_rope_tables(n, MLA_ROPE)
    rope_diff = axial_rope_tables(n, DIFF_HD)
    h = ctx

    def modulated(z, m, g, i):
        return rmsnorm(z, g) * (1.0 + m[:, :, 3 * i + 1]) + m[:, :, 3 * i]

    for l in range(DEPTH):
        last = l == DEPTH - 1
        mod_x = (jax.nn.silu(c) @ ada_w[l] + ada_b[l]).reshape(b, 1, N_ADA, d)
        mod_h = (jax.nn.silu(c_ctx)[None] @ ada_w[l] + ada_b[l]).reshape(1, 1, N_ADA, d)
        ffa = (ffa_w_gate[l], ffa_w_up[l], ffa_w_down[l])
        ffb = (ffb_w_gate[l], ffb_w_up[l], ffb_w_down[l])

        x = x + 0.5 * mod_x[:, :, 2] * swiglu(modulated(x, mod_x, norm_g[l, 0], 0), *ffa)
        h = h + 0.5 * mod_h[:, :, 2] * swiglu(modulated(h, mod_h, norm_g[l, 0], 0), *ffa)

        proj = (w_in[l], mla_q_norm_g[l], mla_kv_norm_g[l], mla_w_uq[l], mla_w_ukv[l])
        sx = mixer_streams(modulated(x, mod_x, norm_g[l, 1], 1), *proj, rope_mla, rope_diff)
        sh = mixer_streams(modulated(h, mod_h, norm_g[l, 1], 1), *proj, None, None)
        lam_init = 0.8 - 0.6 * math.exp(-0.3 * l)
        lq1, lk1, lq2, lk2 = (diff_lambda[l, i].astype(jnp.float32) for i in range(4))
        lam = jnp.exp(jnp.sum(lq1 * lk1)) - jnp.exp(jnp.sum(lq2 * lk2)) + lam_init
        out_params = (lam, lam_init, diff_subln_g[l], pool_proj[l], pool_b[l], pool_scale[l],
                      b_gate[l], w_br_mla[l], w_br_diff[l], w_br_pool[l], w_out[l])
        keys_x = {k: jnp.concatenate([sh[k], sx[k]], axis=1) for k in KEY_NAMES}
        x = x + mod_x[:, :, 5] * mixer_output(sx, keys_x, *out_params)
        if not last:
            h = h + mod_h[:, :, 5] * mixer_output(sh, sh, *out_params)

        x = x + 0.5 * mod_x[:, :, 8] * swiglu(modulated(x, mod_x, norm_g[l, 2], 2), *ffb)
        if not last:
            h = h + 0.5 * mod_h[:, :, 8] * swiglu(modulated(h, mod_h, norm_g[l, 2], 2), *ffb)

    return rmsnorm(x, final_g)
```

```python
import math
import os
import numpy as np
DBG = os.environ.get("KDBG", "")
from contextlib import ExitStack
import concourse.bass as bass
import concourse.mybir as mybir
from concourse.bass_utils import run_bass_kernel_spmd

F32 = mybir.dt.float32
BF16 = mybir.dt.bfloat16
AF = mybir.ActivationFunctionType
ALU = mybir.AluOpType
AX = mybir.AxisListType

D = 1024
KC = 8
FF = 2816
FC = 22
CTX = 256
G = 256
EPS = 1e-6
NH = 8
MLA_SCALE = 96 ** -0.5
DIFF_SCALE = 32 ** -0.5
WIN = 6848
O_CQ, O_CKV, O_DQ, O_DQS, O_DK, O_DKS, O_POOL, O_GATE, O_DV, O_KR, O_KRS = (
    0, 384, 640, 1152, 1664, 2176, 2688, 3200, 6272, 6784, 6816)
POOL_WINDOWS = (2, 4, 8, 16)


class Buf:
    __slots__ = ("w", "rs", "name", "x")

    def __init__(self, name="", excl=False):
        self.w = None
        self.rs = []
        self.name = name
        self.x = excl


DMA_KEYS = ("misc", "adaw0", "adaw1", "wload", "xl0", "xl1", "xs0", "xs1", "cs0", "cs1",
            "st_qd", "st_kd", "st_kr", "st_pl", "st_gt0", "st_gt1", "st_dv", "st_qm", "st_kn", "st_v",
            "op0", "op1", "sto0", "sto1", "pl0", "pl1", "psto0", "psto1", "obl0", "obl1", "gtl0", "gtl1")


class Prog:
    ENG = ("pe", "act", "dve", "pool", "sp")

    def __init__(self, nc, stack, same_engine_raw=True):
        self.nc = nc
        self.stack = stack
        self.cnt = {e: 0 for e in self.ENG}
        self.semsets = [{e: stack.enter_context(nc.semaphore(f"prog{i}_" + e)) for e in self.ENG if e != "sp"}
                        for i in range(3)]
        self.phase = 0
        self.sem = self.semsets[0]
        self.dsem = {}
        self.dcnt = {}
        for key in DMA_KEYS:
            self.dsem[key] = stack.enter_context(nc.semaphore("dma_" + key))
            self.dcnt[key] = 0
        self.waited = {e: {} for e in self.ENG}
        self.same_engine_raw = same_engine_raw
        self.nops = 0
        self.rec = {e: [] for e in self.ENG}

    def dma_sem(self, key):
        assert key in self.dsem, key
        return self.dsem[key]

    def _wait(self, eng, ref):
        w = self.waited[eng]
        if ref[0] == "c":
            _, e2, v, ph = ref
            if ph != self.phase:
                return
            if w.get(e2, 0) >= v:
                return
            w[e2] = v
            sm = self.sem[e2]
        else:
            _, key, v = ref
            k = ("d", key)
            if w.get(k, 0) >= v:
                return
            w[k] = v
            sm = self.dsem[key]
        self.rec[eng].append(lambda e, sm=sm, v=v: e.wait_ge(sm, v))

    def add(self, eng, fn, reads=(), writes=(), dma=None, nodep=False):
        xr = [b for b in reads if b.x]
        if xr:
            reads = [b for b in reads if not b.x]
            writes = list(writes) + [b for b in xr if b not in writes]
        if not nodep:
            raw = set()
            other = set()
            for b in reads:
                if b.w is not None:
                    raw.add(b.w)
            for b in writes:
                if b.w is not None:
                    other.add(b.w)
                for r in b.rs:
                    other.add(r)
            for ref in raw | other:
                if ref[0] == "c" and ref[1] == eng:
                    if not (self.same_engine_raw and ref in raw and eng != "pe"):
                        continue
                self._wait(eng, ref)
        self.nops += 1
        if dma is None:
            self.cnt[eng] += 1
            sm = self.sem[eng]
            self.rec[eng].append(lambda e, fn=fn, sm=sm: fn(e).then_inc(sm, 1))
            ref = ("c", eng, self.cnt[eng], self.phase)
        else:
            sm = self.dma_sem(dma)
            self.dcnt[dma] += 16
            self.rec[eng].append(lambda e, fn=fn, sm=sm: fn(e).then_inc(sm, 16))
            ref = ("d", dma, self.dcnt[dma])
        for b in reads:
            b.rs.append(ref)
        for b in writes:
            b.w = ref
            b.rs = []
        return ref

    def barrier(self, engines=None):
        for e in (engines or self.ENG):
            for e2 in self.ENG:
                if e2 != "sp" and e2 != e and self.cnt[e2] > 0:
                    self._wait(e, ("c", e2, self.cnt[e2], self.phase))
            for key, v in self.dcnt.items():
                if v > 0:
                    self._wait(e, ("d", key, v))

    def flush(self, final=False):
        if final:
            self.barrier(engines=("sp",))
        else:
            self.barrier()
        rec = self.rec
        self.rec = {e: [] for e in self.ENG}
        with self.nc.Block() as block:
            def run(lst):
                def body(e):
                    for f in lst:
                        f(e)
                return body
            block.sync(run(rec["sp"]))
            block.tensor(run(rec["pe"]))
            block.scalar(run(rec["act"]))
            block.vector(run(rec["dve"]))
            block.gpsimd(run(rec["pool"]))
        self.phase += 1
        self.sem = self.semsets[self.phase % 3]
        nxt = self.semsets[(self.phase + 1) % 3]
        for e in self.ENG:
            if e != "sp":
                self.cnt[e] = 0
                self.rec[e].append(lambda eo, sm=nxt[e]: eo.sem_clear(sm))
            for e2 in self.ENG:
                self.waited[e].pop(e2, None)


class Rot:
    def __init__(self, items):
        self.items = items
        self.i = 0

    def next(self):
        it = self.items[self.i % len(self.items)]
        self.i += 1
        return it


def small_layout(L):
    off = {}
    o = 0
    for name, n in (("cc", 16), ("adab", L * 72), ("normg", L * 24), ("bgate", L * 24),
                    ("gq", L * 3), ("gkv", L * 2), ("dlam", L * 128), ("subg", L),
                    ("poolb", L * 4), ("pools", L * 4), ("fg", 8)):
        off[name] = o
        o += n
    return off, o


def build_program(SEQ, layers, final, same_engine_raw=True, debug=False, stop_after=None):
    L = len(layers)
    T = CTX + SEQ
    NG = T // G
    NKC = T // 128
    NQG = SEQ // 512
    SOFF, NS = small_layout(L)

    nc = bass.Bass("TRN2", target_bir_lowering=False)

    def din(name, shape, dt=F32):
        return nc.dram_tensor(name, list(shape), dt, kind="ExternalInput").ap()

    xT = din("xT", [KC, 128, T])
    smallc = din("smallc", [128, NS])
    ada_w = din("ada_w", [L, D, 9 * D])
    ffw = {}
    for ab in ("a", "b"):
        ffw[ab] = (din(f"ff{ab}_wg", [L, D, FF]), din(f"ff{ab}_wu", [L, D, FF]), din(f"ff{ab}_wd", [L, FF, D]))
    w_in = din("w_in_ext", [L, D, WIN])
    w_uq = din("w_uq_ext", [L, 384, 1024])
    w_ukv = din("w_ukv_r", [L, 256, 1024])
    pool_proj = din("pool_proj", [L, 4, 128, 128])
    wbr = [din("w_br_mla", [L, 512, D]), din("w_br_diff", [L, 512, D]), din("w_br_pool", [L, 512, D])]
    w_out = din("w_out", [L, D, D])
    cosT = din("cosT", [128, T])
    sinT = din("sinT", [128, T])
    rcT = din("rcT", [128, 4, T])
    if final:
        outT = nc.dram_tensor("outT", [KC, 128, SEQ], F32, kind="ExternalOutput").ap()
    else:
        outT = nc.dram_tensor("outT", [KC, 128, T], F32, kind="ExternalOutput").ap()

    def dscr(name, shape, dt):
        if debug:
            return nc.dram_tensor(name, list(shape), dt, kind="ExternalOutput").ap()
        return nc.dram_tensor(name, list(shape), dt).ap()

    xres = dscr("xres", [KC, 128, T], F32)
    Qm = dscr("Qm", [768, T], BF16)
    Kmn = dscr("Kmn", [512, T], BF16)
    Kr = dscr("Kr", [32, T], BF16)
    Vm = dscr("Vm", [T, NH, 65], BF16)
    Qd = dscr("Qd", [512, T], BF16)
    Kd = dscr("Kd", [512, T], BF16)
    Vd = dscr("Vd", [T, NH, 65], BF16)
    pin = dscr("pin", [4, 128, T], F32)
    gat = dscr("gat", [24, 128, T], F32)
    omla = dscr("omla", [4, 128, T], BF16)
    odiff = dscr("odiff", [4, 128, T], BF16)
    opool = dscr("opool", [4, 128, T], BF16)

    uid = [0]

    with ExitStack() as top:
        p = Prog(nc, top, same_engine_raw=same_engine_raw)

        def sb(st, shape, dt, name="t"):
            uid[0] += 1
            return st.enter_context(nc.sbuf_tensor(f"{name}_{uid[0]}", list(shape), dt))

        ps = [top.enter_context(nc.psum_tensor(f"ps{i}", [128, 512], F32)) for i in range(8)]
        PB = [Buf(f"psb{i}", excl=True) for i in range(8)]

        def psh(i):
            return ps[i // 2][:, (i % 2) * 256:(i % 2) * 256 + 256], [PB[i // 2]]

        def psf(b):
            return ps[b][:, :], [PB[b]]

        sc = sb(top, [128, NS], F32, "smallc")
        Bsc = Buf("sc")
        ones_bf = sb(top, [128, 128], BF16, "ones_bf")
        ones_f = sb(top, [128, 128], F32, "ones_f")
        Bones = Buf("ones")
        modf = sb(top, [128, L, 72, 2], F32, "modf")
        modA = sb(top, [128, L, 3, 8, 2], F32, "modA")
        modG = sb(top, [128, L, 3, 8, 2], F32, "modG")
        lamt = sb(top, [128, L, 4], F32, "lamt")
        Bmod = Buf("mod")
        epsb = sb(top, [128, 1], F32, "epsb")

        def scs(name, a, b=None):
            o = SOFF[name]
            if b is None:
                return sc[:, o + a:o + a + 1]
            return sc[:, o + a:o + b]

        p.add("sp", lambda e: e.dma_start(out=sc[:], in_=smallc[:, :]), writes=[Bsc], dma="misc")
        p.add("dve", lambda e: e.memset(ones_bf[:], 1.0), writes=[Bones])
        p.add("dve", lambda e: e.memset(ones_f[:], 1.0), writes=[Bones])
        p.add("dve", lambda e: e.memset(epsb[:], EPS), writes=[Bones])

        with ExitStack() as st:
            sct = sb(st, [128, 16], F32, "silu_c")
            Bsct = Buf()
            p.add("act", lambda e: e.activation(out=sct[:], in_=scs("cc", 0, 16), func=AF.Silu),
                  reads=[Bsc], writes=[Bsct])
            awt = [sb(st, [128, 8, 1024], F32, "adaw") for _ in range(2)]
            Baw = [Buf(), Buf()]
            blk = 0
            for li in range(L):
                mps, mpb = psf(0)
                for jb in range(0 if "nomm" in DBG else 9):
                    s = blk % 2
                    blk += 1
                    p.add("sp", lambda e, s=s, li=li, jb=jb: e.dma_start(
                        out=awt[s][:], in_=ada_w[li, :, jb * 1024:(jb + 1) * 1024].rearrange("(k p) m -> p k m", p=128)),
                        writes=[Baw[s]], dma=f"adaw{s}")
                    for jj in range(8):
                        j = jb * 8 + jj
                        for k in range(8):
                            p.add("pe", lambda e, s=s, j=j, jj=jj, k=k: e.matmul(
                                ps[0][:, 2 * j:2 * j + 2], awt[s][:, k, jj * 128:(jj + 1) * 128], sct[:, 2 * k:2 * k + 2],
                                start=(k == 0), stop=(k == 7)),
                                reads=[Baw[s], Bsct], writes=mpb)
                for jx in range(0 if "nomod" in DBG else 2):
                    p.add("dve", lambda e, li=li, jx=jx: e.tensor_tensor(
                        out=modf[:, li, :, jx], in0=ps[0][:, 0:144].rearrange("p (j x) -> p j x", x=2)[:, :, jx],
                        in1=scs("adab", li * 72, li * 72 + 72), op=ALU.add),
                        reads=mpb + [Bsc], writes=[Bmod])
                for i in range(0 if "nomod2" in DBG else 3):
                    for jx in range(2):
                        p.add("dve", lambda e, li=li, i=i, jx=jx: e.scalar_tensor_tensor(
                            out=modA[:, li, i, :, jx], in0=modf[:, li, (3 * i + 1) * 8:(3 * i + 1) * 8 + 8, jx], scalar=1.0,
                            in1=scs("normg", li * 24 + i * 8, li * 24 + i * 8 + 8), op0=ALU.add, op1=ALU.mult),
                            reads=[Bmod, Bsc], writes=[Bmod])
                        p.add("dve", lambda e, li=li, i=i, jx=jx: e.tensor_scalar(
                            out=modG[:, li, i, :, jx], in0=modf[:, li, (3 * i + 2) * 8:(3 * i + 2) * 8 + 8, jx],
                            scalar1=(1.0 if i == 1 else 0.5), scalar2=None, op0=ALU.mult),
                            reads=[Bmod], writes=[Bmod])
                lam_init = 0.8 - 0.6 * math.exp(-0.3 * layers[li])
                tmpl = sb(st, [128, 2, 32], F32, "tmpl")
                sl = sb(st, [128, 2], F32, "sl")
                Bl = Buf()
                o = SOFF["dlam"] + li * 128
                if "nolam" in DBG:
                    continue
                for q in range(2):
                    p.add("dve", lambda e, q=q, o=o, tmpl=tmpl: e.tensor_tensor(
                        out=tmpl[:, q, :], in0=sc[:, o + q * 64:o + q * 64 + 32], in1=sc[:, o + q * 64 + 32:o + q * 64 + 64],
                        op=ALU.mult), reads=[Bsc], writes=[Bl])
                p.add("dve", lambda e, sl=sl, tmpl=tmpl: e.tensor_reduce(out=sl[:], in_=tmpl[:], axis=AX.X, op=ALU.add), reads=[Bl], writes=[Bl])
                p.add("act", lambda e, sl=sl: e.activation(out=sl[:], in_=sl[:], func=AF.Exp), reads=[Bl], writes=[Bl])
                p.add("dve", lambda e, sl=sl: e.tensor_tensor(out=sl[:, 0:1], in0=sl[:, 0:1], in1=sl[:, 1:2], op=ALU.subtract),
                      reads=[Bl], writes=[Bl])
                p.add("dve", lambda e, li=li, lam_init=lam_init, sl=sl: e.tensor_scalar(
                    out=lamt[:, li, 0:1], in0=sl[:, 0:1], scalar1=lam_init, scalar2=-1.0, op0=ALU.add, op1=ALU.mult),
                    reads=[Bl], writes=[Bmod])
                p.add("dve", lambda e, li=li, lam_init=lam_init: e.tensor_scalar(
                    out=lamt[:, li, 1:2], in0=scs("subg", li), scalar1=(1.0 - lam_init), scalar2=None, op0=ALU.mult),
                    reads=[Bsc], writes=[Bmod])
            p.flush()

        def mod_cols(g):
            return 1 if g == 0 else 0

        def emit_norm(st_tiles, xg, Bx, u, Bu, li, i, jx, nchunks, width, scale_ap, shift_ap, N=G):
            sq, Bsq, rstd, Brstd, tmp, Btmp, ssh = st_tiles
            ssap, ssb = psh(ssh)
            for c in range(nchunks):
                s_t, s_b = sq.next()
                p.add("pool", lambda e, c=c, s_t=s_t: e.tensor_tensor(out=s_t[:, 0:N], in0=xg[:, c, 0:N], in1=xg[:, c, 0:N], op=ALU.mult),
                      reads=[Bx], writes=[s_b])
                p.add("pe", lambda e, c=c, s_t=s_t: e.matmul(ssap[:, 0:N], ones_bf[:, :], s_t[:, 0:N], start=(c == 0), stop=(c == nchunks - 1)),
                      reads=[s_b, Bones], writes=ssb)
            p.add("act", lambda e: e.activation(out=rstd[:, 0:N], in_=ssap[:, 0:N], func=AF.Sqrt, scale=1.0 / width, bias=epsb[:, 0:1]),
                  reads=ssb + [Bones], writes=[Brstd])
            p.add("dve", lambda e: e.reciprocal(out=rstd[:, 0:N], in_=rstd[:, 0:N]), reads=[Brstd], writes=[Brstd])
            for c in range(nchunks):
                if shift_ap is None:
                    p.add("dve", lambda e, c=c: e.scalar_tensor_tensor(
                        out=u[:, c, 0:N], in0=xg[:, c, 0:N], scalar=scale_ap(c), in1=rstd[:, 0:N], op0=ALU.mult, op1=ALU.mult),
                        reads=[Bx, Brstd, Bmod, Bsc], writes=[Bu])
                else:
                    t_t, t_b = tmp.next()
                    p.add("dve", lambda e, c=c, t_t=t_t: e.scalar_tensor_tensor(
                        out=t_t[:, 0:N], in0=xg[:, c, 0:N], scalar=scale_ap(c), in1=rstd[:, 0:N], op0=ALU.mult, op1=ALU.mult),
                        reads=[Bx, Brstd, Bmod, Bsc], writes=[t_b])
                    p.add("act", lambda e, c=c, t_t=t_t: e.activation(
                        out=u[:, c, 0:N], in_=t_t[:, 0:N], func=AF.Identity, bias=shift_ap(c), scale=1.0),
                        reads=[t_b, Bmod], writes=[Bu])

        def norm_tiles(st, ssh):
            sqs = [(sb(st, [128, G], BF16, "sq"), Buf()) for _ in range(3)]
            tmps = [(sb(st, [128, G], F32, "ntmp"), Buf()) for _ in range(2)]
            rstd = sb(st, [128, G], F32, "rstd")
            return (Rot(sqs), None, rstd, Buf(), Rot(tmps), None, ssh)

        def load_weights_cast(dst_fn, src_fn, nk, Bw, key="wload"):
            for k in range(0 if "nowl" in DBG else nk):
                p.add("pool", lambda e, k=k: e.dma_start(out=dst_fn(k), in_=src_fn(k)), writes=[Bw], dma=key, nodep=True)

        def phase_ffn(li, ab, i_norm, xsrc, xdst, final_norm):
            Wg, Wu, Wd = ffw[ab]
            with ExitStack() as st:
                wg = sb(st, [128, 8, FF], BF16, "wg")
                wu = sb(st, [128, 8, FF], BF16, "wu")
                wd = sb(st, [128, FC, D], BF16, "wd")
                Bw = Buf()
                load_weights_cast(lambda k: wg[:, k, :], lambda k: Wg[li, k * 128:(k + 1) * 128, :], 8, Bw)
                load_weights_cast(lambda k: wu[:, k, :], lambda k: Wu[li, k * 128:(k + 1) * 128, :], 8, Bw)
                load_weights_cast(lambda k: wd[:, k, :], lambda k: Wd[li, k * 128:(k + 1) * 128, :], FC, Bw)
                xg = [sb(st, [128, 8, G], F32, "xg") for _ in range(2)]
                Bx = [Buf(), Buf()]
                u = [sb(st, [128, 8, G], BF16, "u") for _ in range(2)]
                Bu = [Buf(), Buf()]
                h = sb(st, [128, FC, G], BF16, "h")
                Bh = Buf()
                sil = Rot([(sb(st, [128, G], F32, "sil"), Buf()) for _ in range(2)])
                nt = norm_tiles(st, 0)
                gu_rot = Rot([(4, 5), (6, 7), (8, 9)])
                y_rot = Rot([10, 12, 14])
                if final_norm:
                    yo = [sb(st, [128, 8, G], F32, "yo") for _ in range(2)]
                    Byo = [Buf(), Buf()]
                    nt2 = norm_tiles(st, 2)

                def load_x(g):
                    s = g % 2
                    p.add("sp", lambda e: e.dma_start(out=xg[s][:], in_=xsrc[:, :, g * G:(g + 1) * G].rearrange("c p t -> p c t")),
                          writes=[Bx[s]], dma=f"xl{s}")

                def norm(g):
                    s = g % 2
                    jx = mod_cols(g)
                    emit_norm(nt, xg[s], Bx[s], u[s], Bu[s], li, i_norm, jx, 8, float(D),
                              lambda c: modA[:, li, i_norm, c, jx:jx + 1],
                              lambda c: modf[:, li, (3 * i_norm) * 8 + c, jx:jx + 1])

                def gateup(g):
                    s = g % 2
                    for j in range(FC):
                        hg, hu = gu_rot.next()
                        gap, gb = psh(hg)
                        uap, ub = psh(hu)
                        for (w_t, o_ap, o_b) in ((wg, gap, gb), (wu, uap, ub)):
                            for k in range(8):
                                p.add("pe", lambda e, w_t=w_t, o_ap=o_ap, k=k, j=j: e.matmul(
                                    o_ap, w_t[:, k, j * 128:(j + 1) * 128], u[s][:, k, :], start=(k == 0), stop=(k == 7)),
                                    reads=[Bw, Bu[s]], writes=o_b)
                        s_t, s_b = sil.next()
                        p.add("act", lambda e, s_t=s_t, gap=gap: e.activation(out=s_t[:], in_=gap, func=AF.Silu),
                              reads=gb, writes=[s_b])
                        p.add("dve", lambda e, s_t=s_t, uap=uap, j=j: e.tensor_tensor(out=h[:, j, :], in0=s_t[:], in1=uap, op=ALU.mult),
                              reads=[s_b] + ub, writes=[Bh])

                def down(g):
                    s = g % 2
                    jx = mod_cols(g)
                    for c in range(8):
                        yap, yb = psh(y_rot.next())
                        for j in range(FC):
                            p.add("pe", lambda e, yap=yap, c=c, j=j: e.matmul(
                                yap, wd[:, j, c * 128:(c + 1) * 128], h[:, j, :], start=(j == 0), stop=(j == FC - 1)),
                                reads=[Bw, Bh], writes=yb)
                        p.add("dve", lambda e, yap=yap, c=c: e.scalar_tensor_tensor(
                            out=xg[s][:, c, :], in0=yap, scalar=modG[:, li, i_norm, c, jx:jx + 1], in1=xg[s][:, c, :],
                            op0=ALU.mult, op1=ALU.add), reads=yb + [Bx[s], Bmod], writes=[Bx[s]])

                def store(g):
                    s = g % 2
                    if final_norm:
                        if g == 0:
                            return
                        emit_norm(nt2, xg[s], Bx[s], yo[s], Byo[s], li, 0, 0, 8, float(D),
                                  lambda c: scs("fg", c), None)
                        p.add("sp", lambda e: e.dma_start(out=xdst[:, :, g * G - CTX:(g + 1) * G - CTX].rearrange("c p t -> p c t"), in_=yo[s][:]),
                              reads=[Byo[s]], dma=f"xs{s}")
                    else:
                        p.add("sp", lambda e: e.dma_start(out=xdst[:, :, g * G:(g + 1) * G].rearrange("c p t -> p c t"), in_=xg[s][:]),
                              reads=[Bx[s]], dma=f"xs{s}")

                load_x(0)
                if NG > 1:
                    load_x(1)
                if "nonorm" not in DBG:
                    norm(0)
                for g in range(NG):
                    if "nogu" not in DBG:
                        gateup(g)
                    if g + 1 < NG and "nonorm" not in DBG:
                        norm(g + 1)
                    if "nodown" not in DBG:
                        down(g)
                    store(g)
                    if g + 2 < NG:
                        load_x(g + 2)
                p.flush()

        def phase_proj(li):
            with ExitStack() as st:
                wi = sb(st, [128, 8, WIN], BF16, "w_in")
                wq = sb(st, [128, 3, 1024], BF16, "w_uq")
                wkv = sb(st, [128, 2, 1024], BF16, "w_ukv")
                Bw = Buf()
                load_weights_cast(lambda k: wi[:, k, :], lambda k: w_in[li, k * 128:(k + 1) * 128, :], 8, Bw)
                load_weights_cast(lambda k: wq[:, k, :], lambda k: w_uq[li, k * 128:(k + 1) * 128, :], 3, Bw)
                load_weights_cast(lambda k: wkv[:, k, :], lambda k: w_ukv[li, k * 128:(k + 1) * 128, :], 2, Bw)
                xg = [sb(st, [128, 8, G], F32, "xg") for _ in range(2)]
                Bx = [Buf(), Buf()]
                u = [sb(st, [128, 8, G], BF16, "u") for _ in range(2)]
                Bu = [Buf(), Buf()]
                nt = norm_tiles(st, 0)
                ntq = norm_tiles(st, 2)
                cs = [sb(st, [128, G], F32, "cos") for _ in range(2)]
                sn = [sb(st, [128, G], F32, "sin") for _ in range(2)]
                Bcs = [Buf(), Buf()]
                cq = sb(st, [128, 3, G], F32, "cq")
                Bcq = Buf()
                cqn = sb(st, [128, 3, G], BF16, "cqn")
                Bcqn = Buf()
                ckv = sb(st, [128, 2, G], F32, "ckv")
                Bckv = Buf()
                ckvn = sb(st, [128, 2, G], BF16, "ckvn")
                Bckvn = Buf()
                rt = Rot([((sb(st, [128, G], F32, "rt1"), sb(st, [128, G], F32, "rt2")), Buf()) for _ in range(2)])
                s_qd = sb(st, [128, 4, G], BF16, "s_qd"); B_qd = Buf()
                s_kd = sb(st, [128, 4, G], BF16, "s_kd"); B_kd = Buf()
                s_qm = sb(st, [128, 6, G], BF16, "s_qm"); B_qm = Buf()
                s_kn = sb(st, [128, 4, G], BF16, "s_kn"); B_kn = Buf()
                s_kr = sb(st, [32, G], BF16, "s_kr"); B_kr = Buf()
                s_pl = sb(st, [128, 4, G], F32, "s_pl"); B_pl = Buf()
                s_gt = [sb(st, [128, 8, G], F32, "s_gt") for _ in range(2)]; B_gt = [Buf(), Buf()]
                s_dv = sb(st, [128, 2, NH, 65], BF16, "s_dv"); B_dv = Buf()
                s_v = sb(st, [128, 2, NH, 65], BF16, "s_v"); B_v = Buf()
                p.add("dve", lambda e: e.memset(s_dv[:], 1.0), writes=[B_dv])
                p.add("dve", lambda e: e.memset(s_v[:], 1.0), writes=[B_v])
                brot = Rot([2, 3, 4, 5, 6, 7])

                class _FR:
                    def next(self_):
                        return brot.next()
                frot = _FR()
                ev = [0]

                def evac_copy(out_ap, in_ap, reads, writes):
                    ev[0] += 1
                    if ev[0] % 2:
                        p.add("act", lambda e: e.activation(out=out_ap, in_=in_ap, func=AF.Identity), reads=reads, writes=writes)
                    else:
                        p.add("dve", lambda e: e.tensor_copy(out=out_ap, in_=in_ap), reads=reads, writes=writes)

                def lin(col, s, wt=None, kin=8, rhs=None, Brhs=None, M=128, hp=None):
                    wt = wi if wt is None else wt
                    if hp is None:
                        hp = 2 * brot.next()
                    ap, b = psh(hp)
                    ap = ap[0:M, :]
                    for k in range(kin):
                        r = (u[s][:, k, :] if rhs is None else rhs[:, k, :])
                        p.add("pe", lambda e, ap=ap, k=k, r=r: e.matmul(ap, wt[:, k, col:col + M], r, start=(k == 0), stop=(k == kin - 1)),
                              reads=[Bw, (Bu[s] if Brhs is None else Brhs)], writes=b)
                    return ap, b

                def rope_out(apA, bA, apB, bB, out_ap, Bout, s, M=128):
                    (t1, t2), tb = rt.next()
                    p.add("dve", lambda e: e.tensor_tensor(out=t1[0:M, :], in0=apA, in1=cs[s][0:M, :], op=ALU.mult),
                          reads=bA + [Bcs[s]], writes=[tb])
                    p.add("dve", lambda e: e.tensor_tensor(out=t2[0:M, :], in0=apB, in1=sn[s][0:M, :], op=ALU.mult),
                          reads=bB + [Bcs[s]], writes=[tb])
                    p.add("pool", lambda e: e.tensor_tensor(out=out_ap, in0=t1[0:M, :], in1=t2[0:M, :], op=ALU.add),
                          reads=[tb], writes=[Bout])

                def load_x(g):
                    s = g % 2
                    p.add("sp", lambda e: e.dma_start(out=xg[s][:], in_=xres[:, :, g * G:(g + 1) * G].rearrange("c p t -> p c t")),
                          writes=[Bx[s]], dma=f"xl{s}")
                    p.add("sp", lambda e: e.dma_start(out=cs[s][:], in_=cosT[:, g * G:(g + 1) * G]), writes=[Bcs[s]], dma=f"cs{s}")
                    p.add("sp", lambda e: e.dma_start(out=sn[s][:], in_=sinT[:, g * G:(g + 1) * G]), writes=[Bcs[s]], dma=f"cs{s}")

                def norm(g):
                    s = g % 2
                    jx = mod_cols(g)
                    emit_norm(nt, xg[s], Bx[s], u[s], Bu[s], li, 1, jx, 8, float(D),
                              lambda c: modA[:, li, 1, c, jx:jx + 1],
                              lambda c: modf[:, li, 3 * 8 + c, jx:jx + 1])

                def proj(g):
                    s = g % 2
                    t0 = g * G
                    for c in range(3):
                        ap, b = lin(O_CQ + c * 128, s)
                        evac_copy(cq[:, c, :], ap, b, [Bcq])
                    for c in range(2):
                        ap, b = lin(O_CKV + c * 128, s)
                        evac_copy(ckv[:, c, :], ap, b, [Bckv])
                    emit_norm(ntq, cq, Bcq, cqn, Bcqn, li, 0, 0, 3, 384.0, lambda c: scs("gq", li * 3 + c), None)
                    emit_norm(ntq, ckv, Bckv, ckvn, Bckvn, li, 0, 0, 2, 256.0, lambda c: scs("gkv", li * 2 + c), None)
                    for (oa, ob, stg, Bs, dst, key) in ((O_DQ, O_DQS, s_qd, B_qd, Qd, "st_qd"), (O_DK, O_DKS, s_kd, B_kd, Kd, "st_kd")):
                        for c in range(4):
                            hp = 2 * brot.next()
                            apA, bA = lin(oa + c * 128, s, hp=hp)
                            apB, bB = lin(ob + c * 128, s, hp=hp + 1)
                            rope_out(apA, bA, apB, bB, stg[:, c, :], Bs, s)
                        p.add("sp", lambda e, stg=stg, dst=dst: e.dma_start(
                            out=dst[:, t0:t0 + G].rearrange("(c p) t -> p c t", p=128), in_=stg[:]), reads=[Bs], dma=key)
                    hp = 2 * brot.next()
                    apA, bA = lin(O_KR, s, M=32, hp=hp)
                    apB, bB = lin(O_KRS, s, M=32, hp=hp + 1)
                    rope_out(apA, bA, apB, bB, s_kr[:, :], B_kr, s, M=32)
                    p.add("sp", lambda e: e.dma_start(out=Kr[:, t0:t0 + G], in_=s_kr[:]), reads=[B_kr], dma="st_kr")
                    for c in range(4):
                        ap, b = lin(O_POOL + c * 128, s)
                        evac_copy(s_pl[:, c, :], ap, b, [B_pl])
                    p.add("sp", lambda e: e.dma_start(out=pin[:, :, t0:t0 + G].rearrange("c p t -> p c t"), in_=s_pl[:]),
                          reads=[B_pl], dma="st_pl")
                    for gb3 in range(3):
                        gs = gb3 % 2
                        for c in range(8):
                            j = gb3 * 8 + c
                            ap, b = lin(O_GATE + j * 128, s)
                            p.add("act", lambda e, ap=ap, gs=gs, c=c, j=j: e.activation(
                                out=s_gt[gs][:, c, :], in_=ap, func=AF.Sigmoid, bias=scs("bgate", li * 24 + j), scale=1.0),
                                reads=b + [Bsc], writes=[B_gt[gs]])
                        p.add("sp", lambda e, gs=gs, gb3=gb3: e.dma_start(
                            out=gat[gb3 * 8:(gb3 + 1) * 8, :, t0:t0 + G].rearrange("c p t -> p c t"), in_=s_gt[gs][:]),
                            reads=[B_gt[gs]], dma=f"st_gt{gs}")
                    for sub in range(2):
                        fb = frot.next()
                        fap, fbufs = psf(fb)
                        for k in range(8):
                            p.add("pe", lambda e, fap=fap, k=k, sub=sub: e.matmul(
                                fap, u[s][:, k, sub * 128:(sub + 1) * 128], wi[:, k, O_DV:O_DV + 512], start=(k == 0), stop=(k == 7)),
                                reads=[Bw, Bu[s]], writes=fbufs)
                        evac_copy(s_dv[:, sub, :, 0:64], ps[fb][:, :].rearrange("p (h d) -> p h d", d=64), fbufs, [B_dv])
                    p.add("sp", lambda e: e.dma_start(out=Vd[t0:t0 + G, :, :].rearrange("(s p) h d -> p s h d", p=128), in_=s_dv[:]),
                          reads=[B_dv], dma="st_dv")
                    for c in range(4):
                        ap, b = lin(c * 128, s, wt=wq, kin=3, rhs=cqn, Brhs=Bcqn)
                        evac_copy(s_qm[:, c, :], ap, b, [B_qm])
                    for c in range(2):
                        hp = 2 * brot.next()
                        apA, bA = lin(512 + c * 128, s, wt=wq, kin=3, rhs=cqn, Brhs=Bcqn, hp=hp)
                        apB, bB = lin(768 + c * 128, s, wt=wq, kin=3, rhs=cqn, Brhs=Bcqn, hp=hp + 1)
                        rope_out(apA, bA, apB, bB, s_qm[:, 4 + c, :], B_qm, s)
                    p.add("sp", lambda e: e.dma_start(out=Qm[:, t0:t0 + G].rearrange("(c p) t -> p c t", p=128), in_=s_qm[:]),
                          reads=[B_qm], dma="st_qm")
                    for c in range(4):
                        ap, b = lin(c * 128, s, wt=wkv, kin=2, rhs=ckvn, Brhs=Bckvn)
                        evac_copy(s_kn[:, c, :], ap, b, [B_kn])
                    p.add("sp", lambda e: e.dma_start(out=Kmn[:, t0:t0 + G].rearrange("(c p) t -> p c t", p=128), in_=s_kn[:]),
                          reads=[B_kn], dma="st_kn")
                    for sub in range(2):
                        fb = frot.next()
                        fap, fbufs = psf(fb)
                        for k in range(2):
                            p.add("pe", lambda e, fap=fap, k=k, sub=sub: e.matmul(
                                fap, ckvn[:, k, sub * 128:(sub + 1) * 128], wkv[:, k, 512:1024], start=(k == 0), stop=(k == 1)),
                                reads=[Bw, Bckvn], writes=fbufs)
                        evac_copy(s_v[:, sub, :, 0:64], ps[fb][:, :].rearrange("p (h d) -> p h d", d=64), fbufs, [B_v])
                    p.add("sp", lambda e: e.dma_start(out=Vm[t0:t0 + G, :, :].rearrange("(s p) h d -> p s h d", p=128), in_=s_v[:]),
                          reads=[B_v], dma="st_v")

                load_x(0)
                norm(0)
                for g in range(NG):
                    if g + 1 < NG:
                        load_x(g + 1)
                    proj(g)
                    if g + 1 < NG:
                        norm(g + 1)
                p.flush()

        def phase_att(li):
            with ExitStack() as st:
                Kt = [sb(st, [96, T], BF16, "Kt") for _ in range(2)]
                Qt = [sb(st, [96, T], BF16, "Qt") for _ in range(2)]
                Qt_main = Qt
                Qz = [sb(st, [96, T], BF16, "Qz") for _ in range(2)]
                Vt = [sb(st, [128, NKC, 65], BF16, "Vt") for _ in range(2)]
                Bop = [Buf(), Buf()]
                pt = Rot([(sb(st, [128, 512], BF16, "pt"), Buf()) for _ in range(4)])
                oa = [sb(st, [65, 512], F32, "oa") for _ in range(2)]
                Boa = [Buf(), Buf()]
                rr = [sb(st, [65, 512], F32, "rr") for _ in range(2)]
                Brr = [Buf(), Buf()]
                t1 = sb(st, [64, 512], F32, "t1"); Bt1 = Buf()
                t2 = sb(st, [64, 512], F32, "t2"); Bt2 = Buf()
                sqd = sb(st, [64, 512], BF16, "sqd"); Bsqd = Buf()
                rs = sb(st, [64, 512], F32, "rs"); Brs = Buf()
                ost = Rot([(sb(st, [64, 512], BF16, "ost"), Buf(), f"sto{i}") for i in range(2)])
                srot = Rot([0, 1, 2, 3])
                orot = Rot([4, 5])
                xrot = Rot([6, 7])
                qgroups = [(0, CTX, 0, CTX // 128)] + [(CTX + q * 512, 512, 0, NKC) for q in range(NQG)]
                hs = [1 if "hs1" in DBG else 0]

                def attend(s, qrows, krows, q0, N, kc0, kc1, scale, Qt=None):
                    Qt = Qt_main if Qt is None else Qt
                    ob = orot.next()
                    oap, obufs = psf(ob)
                    LA = 2
                    sb_list = []

                    def s_mm(kc):
                        b = srot.next()
                        sap, sbufs = psf(b)
                        p.add("pe", lambda e: e.matmul(ps[b][:, 0:N], Kt[s][krows[0]:krows[1], kc * 128:(kc + 1) * 128],
                                                       Qt[s][qrows[0]:qrows[1], q0:q0 + N], start=True, stop=True),
                              reads=[Bop[s]], writes=sbufs)
                        sb_list.append((b, sbufs))

                    kcs = list(range(kc0, kc1))
                    for i in range(min(LA, len(kcs))):
                        s_mm(kcs[i])
                    for i, kc in enumerate(kcs):
                        if i + LA < len(kcs):
                            s_mm(kcs[i + LA])
                        b, sbufs = sb_list[i]
                        p_t, p_b = pt.next()
                        p.add("act", lambda e, b=b, p_t=p_t: e.activation(out=p_t[:, 0:N], in_=ps[b][:, 0:N], func=AF.Exp, scale=scale),
                              reads=sbufs, writes=[p_b])
                        p.add("pe", lambda e, p_t=p_t, kc=kc, i=i: e.matmul(ps[ob][0:65, 0:N], Vt[s][:, kc, :], p_t[:, 0:N],
                                                                           start=(i == 0), stop=(i == len(kcs) - 1)),
                              reads=[p_b, Bop[s]], writes=obufs)
                    return ob, obufs

                def finalize(ob, obufs, N, w):
                    p.add("act", lambda e: e.activation(out=oa[w][:, 0:N], in_=ps[ob][0:65, 0:N], func=AF.Identity),
                          reads=obufs, writes=[Boa[w]])
                    p.add("dve", lambda e: e.reciprocal(out=rr[w][64:65, 0:N], in_=oa[w][64:65, 0:N]), reads=[Boa[w]], writes=[Brr[w]])
                    xb = xrot.next()
                    xap, xbufs = psf(xb)
                    p.add("pe", lambda e: e.matmul(ps[xb][0:64, 0:N], ones_f[64:65, 0:64], rr[w][64:65, 0:N], start=True, stop=True),
                          reads=[Brr[w], Bones], writes=xbufs)
                    return xb, xbufs

                for h in range(NH):
                    if "evenonly" in DBG and h % 2 == 1:
                        continue
                    if "mla1" in DBG and h > 0:
                        continue
                    if "nomla" in DBG:
                        continue
                    s = hs[0] % 2
                    hs[0] += 1
                    for (dst, src) in ((Kt[s][0:64, :], Kmn[h * 64:(h + 1) * 64, :]), (Kt[s][64:96, :], Kr[:, :]),
                                       (Qt[s][0:64, :], Qm[h * 64:(h + 1) * 64, :]), (Qt[s][64:96, :], Qm[512 + h * 32:512 + (h + 1) * 32, :])):
                        p.add("sp", lambda e, dst=dst, src=src: e.dma_start(out=dst, in_=src), writes=[Bop[s]], dma=f"op{s}")
                    for c0 in range(0, NKC, 11):
                        c1 = min(NKC, c0 + 11)
                        p.add("sp", lambda e, h=h, s=s, c0=c0, c1=c1: e.dma_start(
                            out=Vt[s][:, c0:c1, :], in_=Vm[c0 * 128:c1 * 128, h, :].rearrange("(c p) d -> p c d", p=128)),
                            writes=[Bop[s]], dma=f"op{s}")
                    for (q0, N, kc0, kc1) in qgroups:
                        ob, obufs = attend(s, (0, 96), (0, 96), q0, N, kc0, kc1, MLA_SCALE)
                        xb, xbufs = finalize(ob, obufs, N, 0)
                        o_t, o_b, o_k = ost.next()
                        p.add("dve", lambda e, xb=xb, o_t=o_t, N=N: e.tensor_tensor(out=o_t[:, 0:N], in0=oa[0][0:64, 0:N], in1=ps[xb][0:64, 0:N], op=ALU.mult),
                              reads=[Boa[0]] + xbufs, writes=[o_b])
                        p.add("sp", lambda e, o_t=o_t, h=h, q0=q0, N=N: e.dma_start(
                            out=omla[h // 2, (h % 2) * 64:(h % 2) * 64 + 64, q0:q0 + N], in_=o_t[:, 0:N]), reads=[o_b], dma=o_k)

                for s in range(2):
                    p.add("dve", lambda e, s=s: e.memset(Qt[s][32:64, :], 0.0), writes=[Bop[s]])
                    p.add("dve", lambda e, s=s: e.memset(Qt[s][64:96, :], 0.0), writes=[Bop[s]])
                    p.add("dve", lambda e, s=s: e.memset(Qz[s][0:32, :], 0.0), writes=[Bop[s]])
                    p.add("dve", lambda e, s=s: e.memset(Qz[s][64:96, :], 0.0), writes=[Bop[s]])
                for h in range(NH):
                    if "evenonly" in DBG or "nodiff" in DBG:
                        continue
                    if "diff1" in DBG and h > 0:
                        continue
                    s = hs[0] % 2
                    hs[0] += 1
                    for (dst, src) in ((Kt[s][0:32, :], Kd[h * 64:h * 64 + 32, :]), (Kt[s][32:64, :], Kd[h * 64 + 32:h * 64 + 64, :]),
                                       (Qt[s][0:32, :], Qd[h * 64:h * 64 + 32, :]), (Qz[s][32:64, :], Qd[h * 64 + 32:h * 64 + 64, :])):
                        p.add("sp", lambda e, dst=dst, src=src: e.dma_start(out=dst, in_=src), writes=[Bop[s]], dma=f"op{s}")
                    for c0 in range(0, NKC, 11):
                        c1 = min(NKC, c0 + 11)
                        p.add("sp", lambda e, h=h, s=s, c0=c0, c1=c1: e.dma_start(
                            out=Vt[s][:, c0:c1, :], in_=Vd[c0 * 128:c1 * 128, h, :].rearrange("(c p) d -> p c d", p=128)),
                            writes=[Bop[s]], dma=f"op{s}")
                    for (q0, N, kc0, kc1) in qgroups:
                        ob1, obufs1 = attend(s, (0, 96), (0, 96), q0, N, kc0, kc1, DIFF_SCALE, Qt=Qt)
                        ob2, obufs2 = attend(s, (0, 96), (0, 96), q0, N, kc0, kc1, DIFF_SCALE, Qt=Qz)
                        if "nofin" in DBG:
                            continue
                        xb1, xbufs1 = finalize(ob1, obufs1, N, 0)
                        xb2, xbufs2 = finalize(ob2, obufs2, N, 1)
                        p.add("dve", lambda e, xb1=xb1, N=N: e.tensor_tensor(out=t1[:, 0:N], in0=oa[0][0:64, 0:N], in1=ps[xb1][0:64, 0:N], op=ALU.mult),
                              reads=[Boa[0]] + xbufs1, writes=[Bt1])
                        p.add("dve", lambda e, xb2=xb2, N=N: e.tensor_tensor(out=t2[:, 0:N], in0=oa[1][0:64, 0:N], in1=ps[xb2][0:64, 0:N], op=ALU.mult),
                              reads=[Boa[1]] + xbufs2, writes=[Bt2])
                        p.add("dve", lambda e, N=N: e.scalar_tensor_tensor(out=t1[:, 0:N], in0=t2[:, 0:N], scalar=lamt[0:64, li, 0:1], in1=t1[:, 0:N],
                                                                         op0=ALU.mult, op1=ALU.add), reads=[Bt1, Bt2, Bmod], writes=[Bt1])
                        p.add("pool", lambda e, N=N: e.tensor_tensor(out=sqd[:, 0:N], in0=t1[:, 0:N], in1=t1[:, 0:N], op=ALU.mult),
                              reads=[Bt1], writes=[Bsqd])
                        xb = xrot.next()
                        xap, xbufs = psf(xb)
                        p.add("pe", lambda e, xb=xb, N=N: e.matmul(ps[xb][0:64, 0:N], ones_bf[0:64, 0:64], sqd[:, 0:N], start=True, stop=True),
                              reads=[Bsqd, Bones], writes=xbufs)
                        p.add("act", lambda e, xb=xb, N=N: e.activation(out=rs[:, 0:N], in_=ps[xb][0:64, 0:N], func=AF.Sqrt, scale=1.0 / 64.0, bias=epsb[0:64, 0:1]),
                              reads=xbufs + [Bones], writes=[Brs])
                        p.add("dve", lambda e, N=N: e.reciprocal(out=rs[:, 0:N], in_=rs[:, 0:N]), reads=[Brs], writes=[Brs])
                        o_t, o_b, o_k = ost.next()
                        p.add("dve", lambda e, o_t=o_t, N=N: e.scalar_tensor_tensor(out=o_t[:, 0:N], in0=t1[:, 0:N], scalar=lamt[0:64, li, 1:2], in1=rs[:, 0:N],
                                                                                  op0=ALU.mult, op1=ALU.mult), reads=[Bt1, Brs, Bmod], writes=[o_b])
                        p.add("sp", lambda e, o_t=o_t, h=h, q0=q0, N=N: e.dma_start(
                            out=odiff[h // 2, (h % 2) * 64:(h % 2) * 64 + 64, q0:q0 + N], in_=o_t[:, 0:N]), reads=[o_b], dma=o_k)
                p.flush()

        def phase_pool(li):
            PW = 512
            HALO = 8
            with ExitStack() as st:
                pp = sb(st, [128, 4, 128], BF16, "pp")
                Bw = Buf()
                p.add("pool", lambda e: e.dma_start(out=pp[:], in_=pool_proj[li].rearrange("g c d -> c g d")), writes=[Bw], dma="wload", nodep=True)
                xin = [sb(st, [128, PW + 2 * HALO], F32, "pxin") for _ in range(2)]
                Bxin = [Buf(), Buf()]
                rcs = [sb(st, [128, PW], F32, "prc") for _ in range(2)]
                sa = sb(st, [128, PW + 2 * HALO], F32, "psa"); sbb = sb(st, [128, PW + 2 * HALO], F32, "psb")
                Bsa = Buf(); Bsb = Buf()
                pl = Rot([(sb(st, [128, PW], BF16, "ppl"), Buf()) for _ in range(2)])
                ost = Rot([(sb(st, [128, PW], BF16, "pos"), Buf(), f"psto{i}") for i in range(2)])
                brot = Rot([0, 1, 2, 3])
                blocks = [(0, CTX, 0, CTX)] + [(CTX, T, CTX + q * PW, PW) for q in range(SEQ // PW)]
                it = 0
                for gi, w in enumerate(POOL_WINDOWS):
                    for (s0, s1, b0, W) in blocks:
                        s = it % 2
                        it += 1
                        lo = max(s0, b0 - HALO)
                        hi = min(s1, b0 + W + HALO)
                        X = xin[s]
                        p.add("dve", lambda e, X=X: e.memset(X[:], 0.0), writes=[Bxin[s]])
                        p.add("sp", lambda e, X=X, gi=gi, lo=lo, hi=hi, b0=b0: e.dma_start(
                            out=X[:, lo - (b0 - HALO):hi - (b0 - HALO)], in_=pin[gi, :, lo:hi]), writes=[Bxin[s]], dma=f"pl{s}")
                        p.add("sp", lambda e, s=s, gi=gi, b0=b0, W=W: e.dma_start(out=rcs[s][:, 0:W], in_=rcT[:, gi, b0:b0 + W]),
                              writes=[Bxin[s]], dma=f"pl{s}")
                        E = PW + 2 * HALO
                        WW = W + 2 * HALO
                        p.add("dve", lambda e, X=X, WW=WW: e.tensor_tensor(out=sa[:, 1:WW], in0=X[:, 0:WW - 1], in1=X[:, 1:WW], op=ALU.add),
                              reads=[Bxin[s]], writes=[Bsa])
                        cur, Bcur, oth, Both = sa, Bsa, sbb, Bsb
                        vlo, vhi = 1, WW
                        sh = 1
                        ww = 2
                        while ww < w:
                            nlo, nhi = vlo + sh, vhi - sh
                            p.add("dve", lambda e, cur=cur, oth=oth, nlo=nlo, nhi=nhi, sh=sh: e.tensor_tensor(
                                out=oth[:, nlo:nhi], in0=cur[:, nlo - sh:nhi - sh], in1=cur[:, nlo + sh:nhi + sh], op=ALU.add),
                                reads=[Bcur], writes=[Both])
                            cur, Bcur, oth, Both = oth, Both, cur, Bcur
                            vlo, vhi = nlo, nhi
                            sh *= 2
                            ww *= 2
                        assert vlo <= HALO and vhi >= HALO + W, (vlo, vhi, w)
                        p.add("dve", lambda e, cur=cur, oth=oth, s=s, W=W: e.tensor_tensor(
                            out=oth[:, HALO:HALO + W], in0=cur[:, HALO:HALO + W], in1=rcs[s][:, 0:W], op=ALU.mult),
                            reads=[Bcur, Bxin[s]], writes=[Both])
                        pl_t, pl_b = pl.next()
                        p.add("dve", lambda e, oth=oth, X=X, pl_t=pl_t, W=W: e.tensor_tensor(
                            out=pl_t[:, 0:W], in0=oth[:, HALO:HALO + W], in1=X[:, HALO:HALO + W], op=ALU.subtract),
                            reads=[Both, Bxin[s]], writes=[pl_b])
                        bk = brot.next()
                        bap, bbufs = psf(bk)
                        p.add("pe", lambda e, bk=bk, gi=gi, pl_t=pl_t, W=W: e.matmul(ps[bk][:, 0:W], pp[:, gi, :], pl_t[:, 0:W], start=True, stop=True),
                              reads=[Bw, pl_b], writes=bbufs)
                        o_t, o_b, o_k = ost.next()
                        p.add("dve", lambda e, bk=bk, o_t=o_t, gi=gi, W=W: e.tensor_scalar(
                            out=o_t[:, 0:W], in0=ps[bk][:, 0:W], scalar1=scs("poolb", li * 4 + gi), scalar2=scs("pools", li * 4 + gi),
                            op0=ALU.add, op1=ALU.mult), reads=bbufs + [Bsc], writes=[o_b])
                        p.add("sp", lambda e, o_t=o_t, gi=gi, b0=b0, W=W: e.dma_start(out=opool[gi, :, b0:b0 + W], in_=o_t[:, 0:W]),
                              reads=[o_b], dma=o_k)
                p.flush()

        def phase_merge(li):
            with ExitStack() as st:
                wb = [sb(st, [128, 4, D], BF16, "wbr") for _ in range(3)]
                wo = sb(st, [128, 8, D], BF16, "wo")
                Bw = Buf()
                for i in range(3):
                    load_weights_cast(lambda k, i=i: wb[i][:, k, :], lambda k, i=i: wbr[i][li, k * 128:(k + 1) * 128, :], 4, Bw)
                load_weights_cast(lambda k: wo[:, k, :], lambda k: w_out[li, k * 128:(k + 1) * 128, :], 8, Bw)
                xg = [sb(st, [128, 8, G], F32, "xg") for _ in range(2)]
                Bx = [Buf(), Buf()]
                ob = [[sb(st, [128, 4, G], BF16, "obr") for _ in range(3)] for _ in range(2)]
                Bob = [Buf(), Buf()]
                gt = [sb(st, [128, 24, G], F32, "gt") for _ in range(2)]
                Bgt = [Buf(), Buf()]
                mg = sb(st, [128, 8, G], BF16, "mg")
                Bmg = Buf()
                mt = Rot([((sb(st, [128, G], F32, "m1"), sb(st, [128, G], F32, "m2"), sb(st, [128, G], F32, "m3")), Buf()) for _ in range(2)])
                prot3 = Rot([(0, 1, 2), (4, 5, 6)])
                yrot = Rot([8, 10, 12, 14])
                srcs = (omla, odiff, opool)

                def load(g):
                    s = g % 2
                    t0 = g * G
                    p.add("sp", lambda e: e.dma_start(out=xg[s][:], in_=xres[:, :, t0:t0 + G].rearrange("c p t -> p c t")),
                          writes=[Bx[s]], dma=f"xl{s}")
                    for i in range(3):
                        p.add("sp", lambda e, i=i: e.dma_start(out=ob[s][i][:], in_=srcs[i][:, :, t0:t0 + G].rearrange("c p t -> p c t")),
                              writes=[Bob[s]], dma=f"obl{s}")
                    p.add("sp", lambda e: e.dma_start(out=gt[s][:], in_=gat[:, :, t0:t0 + G].rearrange("c p t -> p c t")),
                          writes=[Bgt[s]], dma=f"gtl{s}")

                def merge(g):
                    s = g % 2
                    jx = mod_cols(g)
                    t0 = g * G
                    for c in range(8):
                        aps = []
                        hps = prot3.next()
                        for i in range(3):
                            ap, b = psh(hps[i])
                            for k in range(4):
                                p.add("pe", lambda e, ap=ap, i=i, k=k, c=c: e.matmul(ap, wb[i][:, k, c * 128:(c + 1) * 128], ob[s][i][:, k, :],
                                                                                   start=(k == 0), stop=(k == 3)),
                                      reads=[Bw, Bob[s]], writes=b)
                            aps.append((ap, b))
                        (m1, m2, m3), mb = mt.next()
                        for i, m in enumerate((m1, m2, m3)):
                            p.add("dve", lambda e, i=i, m=m, c=c, ap=aps[i][0]: e.tensor_tensor(out=m[:], in0=ap, in1=gt[s][:, i * 8 + c, :], op=ALU.mult),
                                  reads=aps[i][1] + [Bgt[s]], writes=[mb])
                        p.add("pool", lambda e, m1=m1, m2=m2: e.tensor_tensor(out=m1[:], in0=m1[:], in1=m2[:], op=ALU.add), reads=[mb], writes=[mb])
                        p.add("pool", lambda e, m1=m1, m3=m3, c=c: e.tensor_tensor(out=mg[:, c, :], in0=m1[:], in1=m3[:], op=ALU.add),
                              reads=[mb], writes=[Bmg])
                    for c in range(8):
                        yap, yb = psh(yrot.next())
                        for k in range(8):
                            p.add("pe", lambda e, yap=yap, k=k, c=c: e.matmul(yap, wo[:, k, c * 128:(c + 1) * 128], mg[:, k, :], start=(k == 0), stop=(k == 7)),
                                  reads=[Bw, Bmg], writes=yb)
                        p.add("dve", lambda e, yap=yap, c=c: e.scalar_tensor_tensor(
                            out=xg[s][:, c, :], in0=yap, scalar=modf[:, li, 5 * 8 + c, jx:jx + 1], in1=xg[s][:, c, :], op0=ALU.mult, op1=ALU.add),
                            reads=yb + [Bx[s], Bmod], writes=[Bx[s]])
                    p.add("sp", lambda e: e.dma_start(out=xres[:, :, t0:t0 + G].rearrange("c p t -> p c t"), in_=xg[s][:]),
                          reads=[Bx[s]], dma=f"xs{s}")

                load(0)
                for g in range(NG):
                    if g + 1 < NG:
                        load(g + 1)
                    merge(g)
                p.flush()

        class _Stop(Exception):
            pass

        def chk(name):
            if stop_after == name:
                raise _Stop()
        try:
            chk("M")
            for li in range(L):
                last = (li == L - 1)
                if "attonly" not in DBG:
                    phase_ffn(li, "a", 0, xT if li == 0 else xres, xres, False)
                chk("ffa")
                if "onlyffn" not in DBG:
                    if "attonly" not in DBG:
                        phase_proj(li)
                    chk("proj")
                    phase_att(li)
                    chk("att")
                    phase_pool(li)
                    chk("pool")
                    phase_merge(li)
                    chk("merge")
                phase_ffn(li, "b", 2, xres, outT if last else xres, final and last)
        except _Stop:
            dbg = top.enter_context(nc.sbuf_tensor("dbgmod", [128, L * 144 + L * 48 * 2 + L * 4], F32))
            Bd = Buf()
            p.add("dve", lambda e: e.tensor_copy(out=dbg[:, 0:L * 144], in_=modf[:].rearrange("p l j x -> p (l j x)")), reads=[Bmod], writes=[Bd])
            p.add("dve", lambda e: e.tensor_copy(out=dbg[:, L * 144:L * 192], in_=modA[:].rearrange("p l i c x -> p (l i c x)")), reads=[Bmod], writes=[Bd])
            p.add("dve", lambda e: e.tensor_copy(out=dbg[:, L * 192:L * 240], in_=modG[:].rearrange("p l i c x -> p (l i c x)")), reads=[Bmod], writes=[Bd])
            p.add("dve", lambda e: e.tensor_copy(out=dbg[:, L * 240:L * 244], in_=lamt[:].rearrange("p l x -> p (l x)")), reads=[Bmod], writes=[Bd])
            p.add("sp", lambda e: e.dma_start(out=outT[0, :, 0:L * 244], in_=dbg[:]), reads=[Bd], dma="misc")
        p.add("dve", lambda e: e.memset(epsb[:], EPS), writes=[Bones])
        p.flush(final=True)
        nops = p.nops
    return nc, nops


def _swap_half32(a):
    sh = a.shape
    b = a.reshape(*sh[:-1], sh[-1] // 32, 2, 16)
    return np.ascontiguousarray(b[..., ::-1, :]).reshape(sh)


def _rope_tables(SEQ):
    rows = SEQ // 64
    row_ids = np.repeat(np.arange(rows), 64).astype(np.float32)
    col_ids = np.tile(np.arange(64), rows).astype(np.float32)
    inv = (np.float32(10000.0) ** (-np.arange(8, dtype=np.float32) / np.float32(8))).astype(np.float32)
    ang = np.concatenate([row_ids[:, None] * inv, col_ids[:, None] * inv], axis=-1).astype(np.float32)
    c = np.cos(ang).astype(np.float32).T
    s = np.sin(ang).astype(np.float32).T
    T = CTX + SEQ
    cosT = np.ones((128, T), np.float32)
    sinT = np.zeros((128, T), np.float32)
    for pp in range(128):
        i = pp % 32
        cosT[pp, CTX:] = c[i % 16]
        sinT[pp, CTX:] = -s[i % 16] if i < 16 else s[i % 16]
    return cosT, sinT


def _pool_rc(SEQ):
    T = CTX + SEQ
    rc = np.zeros((4, T), np.float32)
    for gi, w in enumerate(POOL_WINDOWS):
        for (s0, n) in ((0, CTX), (CTX, SEQ)):
            t = np.arange(n)
            lo = np.clip(t - w // 2, 0, n)
            hi = np.clip(t + w - w // 2, 0, n)
            rc[gi, s0:s0 + n] = (1.0 / (hi - lo).astype(np.float32)).astype(np.float32)
    return np.ascontiguousarray(np.broadcast_to(rc[None], (128, 4, T)))


def _pp(v):
    v = np.asarray(v, np.float32).reshape(-1, 128)
    return np.ascontiguousarray(v.T)


def prep_shared(inp, layer_ids, SEQ):
    L = len(layer_ids)
    ls = list(layer_ids)
    f = lambda k: np.asarray(inp[k], np.float32)
    w_in = f("w_in")[ls]
    cq, ckv, kr = w_in[..., 0:384], w_in[..., 384:640], w_in[..., 640:672]
    dq, dk, dv = w_in[..., 672:1184], w_in[..., 1184:1696], w_in[..., 1696:2208]
    pl, gt = w_in[..., 2208:2720], w_in[..., 2720:5792]
    w_in_ext = np.concatenate([cq, ckv, dq, _swap_half32(dq), dk, _swap_half32(dk), pl, gt, dv, kr, _swap_half32(kr)], axis=-1)
    assert w_in_ext.shape[-1] == WIN
    wuq = f("mla_w_uq")[ls].reshape(L, 384, NH, 96)
    nope = wuq[..., :64].reshape(L, 384, 512)
    rope = wuq[..., 64:].reshape(L, 384, 256)
    w_uq_ext = np.concatenate([nope, rope, _swap_half32(rope)], axis=-1)
    wukv = f("mla_w_ukv")[ls].reshape(L, 256, NH, 128)
    w_ukv_r = np.concatenate([wukv[..., :64].reshape(L, 256, 512), wukv[..., 64:].reshape(L, 256, 512)], axis=-1)
    cosT, sinT = _rope_tables(SEQ)
    d = {
        "ada_w": np.ascontiguousarray(f("ada_w")[ls]),
        "ffa_wg": np.ascontiguousarray(f("ffa_w_gate")[ls]), "ffa_wu": np.ascontiguousarray(f("ffa_w_up")[ls]),
        "ffa_wd": np.ascontiguousarray(f("ffa_w_down")[ls]),
        "ffb_wg": np.ascontiguousarray(f("ffb_w_gate")[ls]), "ffb_wu": np.ascontiguousarray(f("ffb_w_up")[ls]),
        "ffb_wd": np.ascontiguousarray(f("ffb_w_down")[ls]),
        "w_in_ext": np.ascontiguousarray(w_in_ext), "w_uq_ext": np.ascontiguousarray(w_uq_ext),
        "w_ukv_r": np.ascontiguousarray(w_ukv_r),
        "pool_proj": np.ascontiguousarray(f("pool_proj")[ls]),
        "w_br_mla": np.ascontiguousarray(f("w_br_mla")[ls]), "w_br_diff": np.ascontiguousarray(f("w_br_diff")[ls]),
        "w_br_pool": np.ascontiguousarray(f("w_br_pool")[ls]), "w_out": np.ascontiguousarray(f("w_out")[ls]),
        "cosT": cosT, "sinT": sinT, "rcT": _pool_rc(SEQ),
    }
    SOFF, NS = small_layout(L)
    sm = np.zeros((128, NS), np.float32)

    def put(name, arr):
        arr = np.asarray(arr, np.float32)
        sm[:, SOFF[name]:SOFF[name] + arr.shape[1]] = arr
    put("adab", np.concatenate([_pp(f("ada_b")[l]) for l in ls], axis=1))
    put("normg", np.concatenate([_pp(f("norm_g")[l].reshape(-1)) for l in ls], axis=1))
    put("bgate", np.concatenate([_pp(f("b_gate")[l].reshape(-1)) for l in ls], axis=1))
    put("gq", np.concatenate([_pp(f("mla_q_norm_g")[l]) for l in ls], axis=1))
    put("gkv", np.concatenate([_pp(f("mla_kv_norm_g")[l]) for l in ls], axis=1))
    put("dlam", np.concatenate([np.broadcast_to(f("diff_lambda")[l].reshape(1, 128), (128, 128)) for l in ls], axis=1))
    put("subg", np.stack([np.tile(f("diff_subln_g")[l], 2) for l in ls], axis=1))
    put("poolb", np.concatenate([_pp(f("pool_b")[l].reshape(-1)) for l in ls], axis=1))
    put("pools", np.concatenate([_pp(f("pool_scale")[l]) for l in ls], axis=1))
    put("fg", _pp(f("final_g")))
    return d, sm


def core_inputs(shared, sm, c_b, c_ctx, xT, L):
    SOFF, NS = small_layout(L)
    sm = sm.copy()
    cc = np.stack([_pp(c_b), _pp(c_ctx)], axis=-1).reshape(128, 16)
    sm[:, SOFF["cc"]:SOFF["cc"] + 16] = cc
    d = dict(shared)
    d["smallc"] = sm
    d["xT"] = xT
    return d


def to_fm(tok):
    return np.ascontiguousarray(np.asarray(tok, np.float32).T.reshape(KC, 128, -1))


_PROG_CACHE = {}


def get_prog(SEQ, layers, final):
    key = (SEQ, tuple(layers), final)
    if key not in _PROG_CACHE:
        _PROG_CACHE[key] = build_program(SEQ, list(layers), final)
    return _PROG_CACHE[key][0]


N_CORES = 4


def kernel(**inp):
    x = np.asarray(inp["x"], np.float32)
    ctx = np.asarray(inp["ctx"], np.float32)
    c = np.asarray(inp["c"], np.float32)
    c_ctx = np.asarray(inp["c_ctx"], np.float32)
    B, SEQ, _ = x.shape
    DEPTH = inp["ada_w"].shape[0]
    layers = list(range(DEPTH))
    shared, sm = prep_shared(inp, layers, SEQ)
    nc = get_prog(SEQ, layers, True)
    in_maps = []
    for b in range(B):
        xT = to_fm(np.concatenate([ctx[b], x[b]], axis=0))
        in_maps.append(core_inputs(shared, sm, c[b], c_ctx, xT, DEPTH))
    res = run_bass_kernel_spmd(nc, in_maps, core_ids=list(range(B)))
    out = np.empty((B, SEQ, D), np.float32)
    for b in range(B):
        o = np.asarray(res.results[b]["outT"], np.float32)
        out[b] = o.reshape(D, SEQ).T
    return out
```

```python
import math
import os
import numpy as np
DBG = os.environ.get("KDBG", "")
from contextlib import ExitStack
import concourse.bass as bass
import concourse.mybir as mybir
from concourse.bass_utils import run_bass_kernel_spmd

F32 = mybir.dt.float32
BF16 = mybir.dt.bfloat16
AF = mybir.ActivationFunctionType
ALU = mybir.AluOpType
AX = mybir.AxisListType

D = 1024
KC = 8
FF = 2816
FC = 22
CTX = 256
G = 256
EPS = 1e-6
NH = 8
MLA_SCALE = 96 ** -0.5
DIFF_SCALE = 32 ** -0.5
WIN = 6848
O_CQ, O_CKV, O_DQ, O_DQS, O_DK, O_DKS, O_POOL, O_GATE, O_DV, O_KR, O_KRS = (
    0, 384, 640, 1152, 1664, 2176, 2688, 3200, 6272, 6784, 6816)
POOL_WINDOWS = (2, 4, 8, 16)


class Buf:
    __slots__ = ("w", "rs", "name", "x")

    def __init__(self, name="", excl=False):
        self.w = None
        self.rs = []
        self.name = name
        self.x = excl


DMA_KEYS = ("misc", "adaw0", "adaw1", "wload", "xl0", "xl1", "xs0", "xs1", "cs0", "cs1",
            "st_qd", "st_kd", "st_kr", "st_pl", "st_gt0", "st_gt1", "st_dv", "st_qm", "st_kn", "st_v",
            "op0", "op1", "sto0", "sto1", "pl0", "pl1", "psto0", "psto1", "obl0", "obl1", "gtl0", "gtl1")


class Prog:
    ENG = ("pe", "act", "dve", "pool", "sp")

    def __init__(self, nc, stack, same_engine_raw=True):
        self.nc = nc
        self.stack = stack
        self.cnt = {e: 0 for e in self.ENG}
        self.semsets = [{e: stack.enter_context(nc.semaphore(f"prog{i}_" + e)) for e in self.ENG if e != "sp"}
                        for i in range(3)]
        self.phase = 0
        self.sem = self.semsets[0]
        self.dsem = {}
        self.dcnt = {}
        for key in DMA_KEYS:
            self.dsem[key] = stack.enter_context(nc.semaphore("dma_" + key))
            self.dcnt[key] = 0
        self.waited = {e: {} for e in self.ENG}
        self.same_engine_raw = same_engine_raw
        self.nops = 0
        self.rec = {e: [] for e in self.ENG}

    def dma_sem(self, key):
        assert key in self.dsem, key
        return self.dsem[key]

    def _wait(self, eng, ref):
        w = self.waited[eng]
        if ref[0] == "c":
            _, e2, v, ph = ref
            if ph != self.phase:
                return
            if w.get(e2, 0) >= v:
                return
            w[e2] = v
            sm = self.sem[e2]
        else:
            _, key, v = ref
            k = ("d", key)
            if w.get(k, 0) >= v:
                return
            w[k] = v
            sm = self.dsem[key]
        self.rec[eng].append(lambda e, sm=sm, v=v: e.wait_ge(sm, v))

    def add(self, eng, fn, reads=(), writes=(), dma=None, nodep=False):
        xr = [b for b in reads if b.x]
        if xr:
            reads = [b for b in reads if not b.x]
            writes = list(writes) + [b for b in xr if b not in writes]
        if not nodep:
            raw = set()
            other = set()
            for b in reads:
                if b.w is not None:
                    raw.add(b.w)
            for b in writes:
                if b.w is not None:
                    other.add(b.w)
                for r in b.rs:
                    other.add(r)
            for ref in raw | other:
                if ref[0] == "c" and ref[1] == eng:
                    if not (self.same_engine_raw and ref in raw and eng != "pe"):
                        continue
                self._wait(eng, ref)
        self.nops += 1
        if dma is None:
            self.cnt[eng] += 1
            sm = self.sem[eng]
            self.rec[eng].append(lambda e, fn=fn, sm=sm: fn(e).then_inc(sm, 1))
            ref = ("c", eng, self.cnt[eng], self.phase)
        else:
            sm = self.dma_sem(dma)
            self.dcnt[dma] += 16
            self.rec[eng].append(lambda e, fn=fn, sm=sm: fn(e).then_inc(sm, 16))
            ref = ("d", dma, self.dcnt[dma])
        for b in reads:
            b.rs.append(ref)
        for b in writes:
            b.w = ref
            b.rs = []
        return ref

    def barrier(self, engines=None):
        for e in (engines or self.ENG):
            for e2 in self.ENG:
                if e2 != "sp" and e2 != e and self.cnt[e2] > 0:
                    self._wait(e, ("c", e2, self.cnt[e2], self.phase))
            for key, v in self.dcnt.items():
                if v > 0:
                    self._wait(e, ("d", key, v))

    def flush(self, final=False):
        if final:
            self.barrier(engines=("sp",))
        else:
            self.barrier()
        rec = self.rec
        self.rec = {e: [] for e in self.ENG}
        with self.nc.Block() as block:
            def run(lst):
                def body(e):
                    for f in lst:
                        f(e)
                return body
            block.sync(run(rec["sp"]))
            block.tensor(run(rec["pe"]))
            block.scalar(run(rec["act"]))
            block.vector(run(rec["dve"]))
            block.gpsimd(run(rec["pool"]))
        self.phase += 1
        self.sem = self.semsets[self.phase % 3]
        nxt = self.semsets[(self.phase + 1) % 3]
        for e in self.ENG:
            if e != "sp":
                self.cnt[e] = 0
                self.rec[e].append(lambda eo, sm=nxt[e]: eo.sem_clear(sm))
            for e2 in self.ENG:
                self.waited[e].pop(e2, None)


class Rot:
    def __init__(self, items):
        self.items = items
        self.i = 0

    def next(self):
        it = self.items[self.i % len(self.items)]
        self.i += 1
        return it


def small_layout(L):
    off = {}
    o = 0
    for name, n in (("cc", 16), ("adab", L * 72), ("normg", L * 24), ("bgate", L * 24),
                    ("gq", L * 3), ("gkv", L * 2), ("dlam", L * 128), ("subg", L),
                    ("poolb", L * 4), ("pools", L * 4), ("fg", 8)):
        off[name] = o
        o += n
    return off, o


def build_program(SEQ, layers, final, same_engine_raw=True, debug=False, stop_after=None):
    L = len(layers)
    T = CTX + SEQ
    NG = T // G
    NKC = T // 128
    NQG = SEQ // 512
    SOFF, NS = small_layout(L)

    nc = bass.Bass("TRN2", target_bir_lowering=False)

    def din(name, shape, dt=F32):
        return nc.dram_tensor(name, list(shape), dt, kind="ExternalInput").ap()

    xT = din("xT", [KC, 128, T])
    smallc = din("smallc", [128, NS])
    ada_w = din("ada_w", [L, D, 9 * D])
    ffw = {}
    for ab in ("a", "b"):
        ffw[ab] = (din(f"ff{ab}_wg", [L, D, FF]), din(f"ff{ab}_wu", [L, D, FF]), din(f"ff{ab}_wd", [L, FF, D]))
    w_in = din("w_in_ext", [L, D, WIN])
    w_uq = din("w_uq_ext", [L, 384, 1024])
    w_ukv = din("w_ukv_r", [L, 256, 1024])
    pool_proj = din("pool_proj", [L, 4, 128, 128])
    wbr = [din("w_br_mla", [L, 512, D]), din("w_br_diff", [L, 512, D]), din("w_br_pool", [L, 512, D])]
    w_out = din("w_out", [L, D, D])
    cosT = din("cosT", [128, T])
    sinT = din("sinT", [128, T])
    rcT = din("rcT", [128, 4, T])
    if final:
        outT = nc.dram_tensor("outT", [KC, 128, SEQ], F32, kind="ExternalOutput").ap()
    else:
        outT = nc.dram_tensor("outT", [KC, 128, T], F32, kind="ExternalOutput").ap()

    def dscr(name, shape, dt):
        if debug:
            return nc.dram_tensor(name, list(shape), dt, kind="ExternalOutput").ap()
        return nc.dram_tensor(name, list(shape), dt).ap()

    xres = dscr("xres", [KC, 128, T], F32)
    Qm = dscr("Qm", [768, T], BF16)
    Kmn = dscr("Kmn", [512, T], BF16)
    Kr = dscr("Kr", [32, T], BF16)
    Vm = dscr("Vm", [T, NH, 65], BF16)
    Qd = dscr("Qd", [512, T], BF16)
    Kd = dscr("Kd", [512, T], BF16)
    Vd = dscr("Vd", [T, NH, 65], BF16)
    pin = dscr("pin", [4, 128, T], F32)
    gat = dscr("gat", [24, 128, T], F32)
    omla = dscr("omla", [4, 128, T], BF16)
    odiff = dscr("odiff", [4, 128, T], BF16)
    opool = dscr("opool", [4, 128, T], BF16)

    uid = [0]

    with ExitStack() as top:
        p = Prog(nc, top, same_engine_raw=same_engine_raw)

        def sb(st, shape, dt, name="t"):
            uid[0] += 1
            return st.enter_context(nc.sbuf_tensor(f"{name}_{uid[0]}", list(shape), dt))

        ps = [top.enter_context(nc.psum_tensor(f"ps{i}", [128, 512], F32)) for i in range(8)]
        PB = [Buf(f"psb{i}", excl=True) for i in range(8)]

        def psh(i):
            return ps[i // 2][:, (i % 2) * 256:(i % 2) * 256 + 256], [PB[i // 2]]

        def psf(b):
            return ps[b][:, :], [PB[b]]

        sc = sb(top, [128, NS], F32, "smallc")
        Bsc = Buf("sc")
        ones_bf = sb(top, [128, 128], BF16, "ones_bf")
        ones_f = sb(top, [128, 128], F32, "ones_f")
        Bones = Buf("ones")
        modf = sb(top, [128, L, 72, 2], F32, "modf")
        modA = sb(top, [128, L, 3, 8, 2], F32, "modA")
        modG = sb(top, [128, L, 3, 8, 2], F32, "modG")
        lamt = sb(top, [128, L, 4], F32, "lamt")
        Bmod = Buf("mod")
        epsb = sb(top, [128, 1], F32, "epsb")

        def scs(name, a, b=None):
            o = SOFF[name]
            if b is None:
                return sc[:, o + a:o + a + 1]
            return sc[:, o + a:o + b]

        p.add("sp", lambda e: e.dma_start(out=sc[:], in_=smallc[:, :]), writes=[Bsc], dma="misc")
        p.add("dve", lambda e: e.memset(ones_bf[:], 1.0), writes=[Bones])
        p.add("dve", lambda e: e.memset(ones_f[:], 1.0), writes=[Bones])
        p.add("dve", lambda e: e.memset(epsb[:], EPS), writes=[Bones])

        with ExitStack() as st:
            sct = sb(st, [128, 16], F32, "silu_c")
            Bsct = Buf()
            p.add("act", lambda e: e.activation(out=sct[:], in_=scs("cc", 0, 16), func=AF.Silu),
                  reads=[Bsc], writes=[Bsct])
            awt = [sb(st, [128, 8, 1024], F32, "adaw") for _ in range(2)]
            Baw = [Buf(), Buf()]
            blk = 0
            for li in range(L):
                mps, mpb = psf(0)
                for jb in range(0 if "nomm" in DBG else 9):
                    s = blk % 2
                    blk += 1
                    p.add("sp", lambda e, s=s, li=li, jb=jb: e.dma_start(
                        out=awt[s][:], in_=ada_w[li, :, jb * 1024:(jb + 1) * 1024].rearrange("(k p) m -> p k m", p=128)),
                        writes=[Baw[s]], dma=f"adaw{s}")
                    for jj in range(8):
                        j = jb * 8 + jj
                        for k in range(8):
                            p.add("pe", lambda e, s=s, j=j, jj=jj, k=k: e.matmul(
                                ps[0][:, 2 * j:2 * j + 2], awt[s][:, k, jj * 128:(jj + 1) * 128], sct[:, 2 * k:2 * k + 2],
                                start=(k == 0), stop=(k == 7)),
                                reads=[Baw[s], Bsct], writes=mpb)
                for jx in range(0 if "nomod" in DBG else 2):
                    p.add("dve", lambda e, li=li, jx=jx: e.tensor_tensor(
                        out=modf[:, li, :, jx], in0=ps[0][:, 0:144].rearrange("p (j x) -> p j x", x=2)[:, :, jx],
                        in1=scs("adab", li * 72, li * 72 + 72), op=ALU.add),
                        reads=mpb + [Bsc], writes=[Bmod])
                for i in range(0 if "nomod2" in DBG else 3):
                    for jx in range(2):
                        p.add("dve", lambda e, li=li, i=i, jx=jx: e.scalar_tensor_tensor(
                            out=modA[:, li, i, :, jx], in0=modf[:, li, (3 * i + 1) * 8:(3 * i + 1) * 8 + 8, jx], scalar=1.0,
                            in1=scs("normg", li * 24 + i * 8, li * 24 + i * 8 + 8), op0=ALU.add, op1=ALU.mult),
                            reads=[Bmod, Bsc], writes=[Bmod])
                        p.add("dve", lambda e, li=li, i=i, jx=jx: e.tensor_scalar(
                            out=modG[:, li, i, :, jx], in0=modf[:, li, (3 * i + 2) * 8:(3 * i + 2) * 8 + 8, jx],
                            scalar1=(1.0 if i == 1 else 0.5), scalar2=None, op0=ALU.mult),
                            reads=[Bmod], writes=[Bmod])
                lam_init = 0.8 - 0.6 * math.exp(-0.3 * layers[li])
                tmpl = sb(st, [128, 2, 32], F32, "tmpl")
                sl = sb(st, [128, 2], F32, "sl")
                Bl = Buf()
                o = SOFF["dlam"] + li * 128
                if "nolam" in DBG:
                    continue
                for q in range(2):
                    p.add("dve", lambda e, q=q, o=o, tmpl=tmpl: e.tensor_tensor(
                        out=tmpl[:, q, :], in0=sc[:, o + q * 64:o + q * 64 + 32], in1=sc[:, o + q * 64 + 32:o + q * 64 + 64],
                        op=ALU.mult), reads=[Bsc], writes=[Bl])
                p.add("dve", lambda e, sl=sl, tmpl=tmpl: e.tensor_reduce(out=sl[:], in_=tmpl[:], axis=AX.X, op=ALU.add), reads=[Bl], writes=[Bl])
                p.add("act", lambda e, sl=sl: e.activation(out=sl[:], in_=sl[:], func=AF.Exp), reads=[Bl], writes=[Bl])
                p.add("dve", lambda e, sl=sl: e.tensor_tensor(out=sl[:, 0:1], in0=sl[:, 0:1], in1=sl[:, 1:2], op=ALU.subtract),
                      reads=[Bl], writes=[Bl])
                p.add("dve", lambda e, li=li, lam_init=lam_init, sl=sl: e.tensor_scalar(
                    out=lamt[:, li, 0:1], in0=sl[:, 0:1], scalar1=lam_init, scalar2=-1.0, op0=ALU.add, op1=ALU.mult),
                    reads=[Bl], writes=[Bmod])
                p.add("dve", lambda e, li=li, lam_init=lam_init: e.tensor_scalar(
                    out=lamt[:, li, 1:2], in0=scs("subg", li), scalar1=(1.0 - lam_init), scalar2=None, op0=ALU.mult),
                    reads=[Bsc], writes=[Bmod])
            p.flush()

        def mod_cols(g):
            return 1 if g == 0 else 0

        def emit_norm(st_tiles, xg, Bx, u, Bu, li, i, jx, nchunks, width, scale_ap, shift_ap, N=G):
            sq, Bsq, rstd, Brstd, tmp, Btmp, ssh = st_tiles
            ssap, ssb = psh(ssh)
            for c in range(nchunks):
                s_t, s_b = sq.next()
                p.add("pool", lambda e, c=c, s_t=s_t: e.tensor_tensor(out=s_t[:, 0:N], in0=xg[:, c, 0:N], in1=xg[:, c, 0:N], op=ALU.mult),
                      reads=[Bx], writes=[s_b])
                p.add("pe", lambda e, c=c, s_t=s_t: e.matmul(ssap[:, 0:N], ones_bf[:, :], s_t[:, 0:N], start=(c == 0), stop=(c == nchunks - 1)),
                      reads=[s_b, Bones], writes=ssb)
            p.add("act", lambda e: e.activation(out=rstd[:, 0:N], in_=ssap[:, 0:N], func=AF.Sqrt, scale=1.0 / width, bias=epsb[:, 0:1]),
                  reads=ssb + [Bones], writes=[Brstd])
            p.add("dve", lambda e: e.reciprocal(out=rstd[:, 0:N], in_=rstd[:, 0:N]), reads=[Brstd], writes=[Brstd])
            for c in range(nchunks):
                if shift_ap is None:
                    p.add("dve", lambda e, c=c: e.scalar_tensor_tensor(
                        out=u[:, c, 0:N], in0=xg[:, c, 0:N], scalar=scale_ap(c), in1=rstd[:, 0:N], op0=ALU.mult, op1=ALU.mult),
                        reads=[Bx, Brstd, Bmod, Bsc], writes=[Bu])
                else:
                    t_t, t_b = tmp.next()
                    p.add("dve", lambda e, c=c, t_t=t_t: e.scalar_tensor_tensor(
                        out=t_t[:, 0:N], in0=xg[:, c, 0:N], scalar=scale_ap(c), in1=rstd[:, 0:N], op0=ALU.mult, op1=ALU.mult),
                        reads=[Bx, Brstd, Bmod, Bsc], writes=[t_b])
                    p.add("act", lambda e, c=c, t_t=t_t: e.activation(
                        out=u[:, c, 0:N], in_=t_t[:, 0:N], func=AF.Identity, bias=shift_ap(c), scale=1.0),
                        reads=[t_b, Bmod], writes=[Bu])

        def norm_tiles(st, ssh):
            sqs = [(sb(st, [128, G], BF16, "sq"), Buf()) for _ in range(3)]
            tmps = [(sb(st, [128, G], F32, "ntmp"), Buf()) for _ in range(2)]
            rstd = sb(st, [128, G], F32, "rstd")
            return (Rot(sqs), None, rstd, Buf(), Rot(tmps), None, ssh)

        def load_weights_cast(dst_fn, src_fn, nk, Bw, key="wload"):
            for k in range(0 if "nowl" in DBG else nk):
                p.add("pool", lambda e, k=k: e.dma_start(out=dst_fn(k), in_=src_fn(k)), writes=[Bw], dma=key, nodep=True)

        def phase_ffn(li, ab, i_norm, xsrc, xdst, final_norm):
            Wg, Wu, Wd = ffw[ab]
            with ExitStack() as st:
                wg = sb(st, [128, 8, FF], BF16, "wg")
                wu = sb(st, [128, 8, FF], BF16, "wu")
                wd = sb(st, [128, FC, D], BF16, "wd")
                Bw = Buf()
                load_weights_cast(lambda k: wg[:, k, :], lambda k: Wg[li, k * 128:(k + 1) * 128, :], 8, Bw)
                load_weights_cast(lambda k: wu[:, k, :], lambda k: Wu[li, k * 128:(k + 1) * 128, :], 8, Bw)
                load_weights_cast(lambda k: wd[:, k, :], lambda k: Wd[li, k * 128:(k + 1) * 128, :], FC, Bw)
                xg = [sb(st, [128, 8, G], F32, "xg") for _ in range(2)]
                Bx = [Buf(), Buf()]
                u = [sb(st, [128, 8, G], BF16, "u") for _ in range(2)]
                Bu = [Buf(), Buf()]
                h = sb(st, [128, FC, G], BF16, "h")
                Bh = Buf()
                sil = Rot([(sb(st, [128, G], F32, "sil"), Buf()) for _ in range(2)])
                nt = norm_tiles(st, 0)
                gu_rot = Rot([(4, 5), (6, 7), (8, 9)])
                y_rot = Rot([10, 12, 14])
                if final_norm:
                    yo = [sb(st, [128, 8, G], F32, "yo") for _ in range(2)]
                    Byo = [Buf(), Buf()]
                    nt2 = norm_tiles(st, 2)

                def load_x(g):
                    s = g % 2
                    p.add("sp", lambda e: e.dma_start(out=xg[s][:], in_=xsrc[:, :, g * G:(g + 1) * G].rearrange("c p t -> p c t")),
                          writes=[Bx[s]], dma=f"xl{s}")

                def norm(g):
                    s = g % 2
                    jx = mod_cols(g)
                    emit_norm(nt, xg[s], Bx[s], u[s], Bu[s], li, i_norm, jx, 8, float(D),
                              lambda c: modA[:, li, i_norm, c, jx:jx + 1],
                              lambda c: modf[:, li, (3 * i_norm) * 8 + c, jx:jx + 1])

                def gateup(g):
                    s = g % 2
                    for j in range(FC):
                        hg, hu = gu_rot.next()
                        gap, gb = psh(hg)
                        uap, ub = psh(hu)
                        for (w_t, o_ap, o_b) in ((wg, gap, gb), (wu, uap, ub)):
                            for k in range(8):
                                p.add("pe", lambda e, w_t=w_t, o_ap=o_ap, k=k, j=j: e.matmul(
                                    o_ap, w_t[:, k, j * 128:(j + 1) * 128], u[s][:, k, :], start=(k == 0), stop=(k == 7)),
                                    reads=[Bw, Bu[s]], writes=o_b)
                        s_t, s_b = sil.next()
                        p.add("act", lambda e, s_t=s_t, gap=gap: e.activation(out=s_t[:], in_=gap, func=AF.Silu),
                              reads=gb, writes=[s_b])
                        p.add("dve", lambda e, s_t=s_t, uap=uap, j=j: e.tensor_tensor(out=h[:, j, :], in0=s_t[:], in1=uap, op=ALU.mult),
                              reads=[s_b] + ub, writes=[Bh])

                def down(g):
                    s = g % 2
                    jx = mod_cols(g)
                    for c in range(8):
                        yap, yb = psh(y_rot.next())
                        for j in range(FC):
                            p.add("pe", lambda e, yap=yap, c=c, j=j: e.matmul(
                                yap, wd[:, j, c * 128:(c + 1) * 128], h[:, j, :], start=(j == 0), stop=(j == FC - 1)),
                                reads=[Bw, Bh], writes=yb)
                        p.add("dve", lambda e, yap=yap, c=c: e.scalar_tensor_tensor(
                            out=xg[s][:, c, :], in0=yap, scalar=modG[:, li, i_norm, c, jx:jx + 1], in1=xg[s][:, c, :],
                            op0=ALU.mult, op1=ALU.add), reads=yb + [Bx[s], Bmod], writes=[Bx[s]])

                def store(g):
                    s = g % 2
                    if final_norm:
                        if g == 0:
                            return
                        emit_norm(nt2, xg[s], Bx[s], yo[s], Byo[s], li, 0, 0, 8, float(D),
                                  lambda c: scs("fg", c), None)
                        p.add("sp", lambda e: e.dma_start(out=xdst[:, :, g * G - CTX:(g + 1) * G - CTX].rearrange("c p t -> p c t"), in_=yo[s][:]),
                              reads=[Byo[s]], dma=f"xs{s}")
                    else:
                        p.add("sp", lambda e: e.dma_start(out=xdst[:, :, g * G:(g + 1) * G].rearrange("c p t -> p c t"), in_=xg[s][:]),
                              reads=[Bx[s]], dma=f"xs{s}")

                load_x(0)
                if NG > 1:
                    load_x(1)
                if "nonorm" not in DBG:
                    norm(0)
                for g in range(NG):
                    if "nogu" not in DBG:
                        gateup(g)
                    if g + 1 < NG and "nonorm" not in DBG:
                        norm(g + 1)
                    if "nodown" not in DBG:
                        down(g)
                    store(g)
                    if g + 2 < NG:
                        load_x(g + 2)
                p.flush()

        def phase_proj(li):
            with ExitStack() as st:
                wi = sb(st, [128, 8, WIN], BF16, "w_in")
                wq = sb(st, [128, 3, 1024], BF16, "w_uq")
                wkv = sb(st, [128, 2, 1024], BF16, "w_ukv")
                Bw = Buf()
                load_weights_cast(lambda k: wi[:, k, :], lambda k: w_in[li, k * 128:(k + 1) * 128, :], 8, Bw)
                load_weights_cast(lambda k: wq[:, k, :], lambda k: w_uq[li, k * 128:(k + 1) * 128, :], 3, Bw)
                load_weights_cast(lambda k: wkv[:, k, :], lambda k: w_ukv[li, k * 128:(k + 1) * 128, :], 2, Bw)
                xg = [sb(st, [128, 8, G], F32, "xg") for _ in range(2)]
                Bx = [Buf(), Buf()]
                u = [sb(st, [128, 8, G], BF16, "u") for _ in range(2)]
                Bu = [Buf(), Buf()]
                nt = norm_tiles(st, 0)
                ntq = norm_tiles(st, 2)
                cs = [sb(st, [128, G], F32, "cos") for _ in range(2)]
                sn = [sb(st, [128, G], F32, "sin") for _ in range(2)]
                Bcs = [Buf(), Buf()]
                cq = sb(st, [128, 3, G], F32, "cq")
                Bcq = Buf()
                cqn = sb(st, [128, 3, G], BF16, "cqn")
                Bcqn = Buf()
                ckv = sb(st, [128, 2, G], F32, "ckv")
                Bckv = Buf()
                ckvn = sb(st, [128, 2, G], BF16, "ckvn")
                Bckvn = Buf()
                rt = Rot([((sb(st, [128, G], F32, "rt1"), sb(st, [128, G], F32, "rt2")), Buf()) for _ in range(2)])
                s_qd = sb(st, [128, 4, G], BF16, "s_qd"); B_qd = Buf()
                s_kd = sb(st, [128, 4, G], BF16, "s_kd"); B_kd = Buf()
                s_qm = sb(st, [128, 6, G], BF16, "s_qm"); B_qm = Buf()
                s_kn = sb(st, [128, 4, G], BF16, "s_kn"); B_kn = Buf()
                s_kr = sb(st, [32, G], BF16, "s_kr"); B_kr = Buf()
                s_pl = sb(st, [128, 4, G], F32, "s_pl"); B_pl = Buf()
                s_gt = [sb(st, [128, 8, G], F32, "s_gt") for _ in range(2)]; B_gt = [Buf(), Buf()]
                s_dv = sb(st, [128, 2, NH, 65], BF16, "s_dv"); B_dv = Buf()
                s_v = sb(st, [128, 2, NH, 65], BF16, "s_v"); B_v = Buf()
                p.add("dve", lambda e: e.memset(s_dv[:], 1.0), writes=[B_dv])
                p.add("dve", lambda e: e.memset(s_v[:], 1.0), writes=[B_v])
                brot = Rot([2, 3, 4, 5, 6, 7])

                class _FR:
                    def next(self_):
                        return brot.next()
                frot = _FR()
                ev = [0]

                def evac_copy(out_ap, in_ap, reads, writes):
                    ev[0] += 1
                    if ev[0] % 2:
                        p.add("act", lambda e: e.activation(out=out_ap, in_=in_ap, func=AF.Identity), reads=reads, writes=writes)
                    else:
                        p.add("dve", lambda e: e.tensor_copy(out=out_ap, in_=in_ap), reads=reads, writes=writes)

                def lin(col, s, wt=None, kin=8, rhs=None, Brhs=None, M=128, hp=None):
                    wt = wi if wt is None else wt
                    if hp is None:
                        hp = 2 * brot.next()
                    ap, b = psh(hp)
                    ap = ap[0:M, :]
                    for k in range(kin):
                        r = (u[s][:, k, :] if rhs is None else rhs[:, k, :])
                        p.add("pe", lambda e, ap=ap, k=k, r=r: e.matmul(ap, wt[:, k, col:col + M], r, start=(k == 0), stop=(k == kin - 1)),
                              reads=[Bw, (Bu[s] if Brhs is None else Brhs)], writes=b)
                    return ap, b

                def rope_out(apA, bA, apB, bB, out_ap, Bout, s, M=128):
                    (t1, t2), tb = rt.next()
                    p.add("dve", lambda e: e.tensor_tensor(out=t1[0:M, :], in0=apA, in1=cs[s][0:M, :], op=ALU.mult),
                          reads=bA + [Bcs[s]], writes=[tb])
                    p.add("dve", lambda e: e.tensor_tensor(out=t2[0:M, :], in0=apB, in1=sn[s][0:M, :], op=ALU.mult),
                          reads=bB + [Bcs[s]], writes=[tb])
                    p.add("pool", lambda e: e.tensor_tensor(out=out_ap, in0=t1[0:M, :], in1=t2[0:M, :], op=ALU.add),
                          reads=[tb], writes=[Bout])

                def load_x(g):
                    s = g % 2
                    p.add("sp", lambda e: e.dma_start(out=xg[s][:], in_=xres[:, :, g * G:(g + 1) * G].rearrange("c p t -> p c t")),
                          writes=[Bx[s]], dma=f"xl{s}")
                    p.add("sp", lambda e: e.dma_start(out=cs[s][:], in_=cosT[:, g * G:(g + 1) * G]), writes=[Bcs[s]], dma=f"cs{s}")
                    p.add("sp", lambda e: e.dma_start(out=sn[s][:], in_=sinT[:, g * G:(g + 1) * G]), writes=[Bcs[s]], dma=f"cs{s}")

                def norm(g):
                    s = g % 2
                    jx = mod_cols(g)
                    emit_norm(nt, xg[s], Bx[s], u[s], Bu[s], li, 1, jx, 8, float(D),
                              lambda c: modA[:, li, 1, c, jx:jx + 1],
                              lambda c: modf[:, li, 3 * 8 + c, jx:jx + 1])

                def proj(g):
                    s = g % 2
                    t0 = g * G
                    for c in range(3):
                        ap, b = lin(O_CQ + c * 128, s)
                        evac_copy(cq[:, c, :], ap, b, [Bcq])
                    for c in range(2):
                        ap, b = lin(O_CKV + c * 128, s)
                        evac_copy(ckv[:, c, :], ap, b, [Bckv])
                    emit_norm(ntq, cq, Bcq, cqn, Bcqn, li, 0, 0, 3, 384.0, lambda c: scs("gq", li * 3 + c), None)
                    emit_norm(ntq, ckv, Bckv, ckvn, Bckvn, li, 0, 0, 2, 256.0, lambda c: scs("gkv", li * 2 + c), None)
                    for (oa, ob, stg, Bs, dst, key) in ((O_DQ, O_DQS, s_qd, B_qd, Qd, "st_qd"), (O_DK, O_DKS, s_kd, B_kd, Kd, "st_kd")):
                        for c in range(4):
                            hp = 2 * brot.next()
                            apA, bA = lin(oa + c * 128, s, hp=hp)
                            apB, bB = lin(ob + c * 128, s, hp=hp + 1)
                            rope_out(apA, bA, apB, bB, stg[:, c, :], Bs, s)
                        p.add("sp", lambda e, stg=stg, dst=dst: e.dma_start(
                            out=dst[:, t0:t0 + G].rearrange("(c p) t -> p c t", p=128), in_=stg[:]), reads=[Bs], dma=key)
                    hp = 2 * brot.next()
                    apA, bA = lin(O_KR, s, M=32, hp=hp)
                    apB, bB = lin(O_KRS, s, M=32, hp=hp + 1)
                    rope_out(apA, bA, apB, bB, s_kr[:, :], B_kr, s, M=32)
                    p.add("sp", lambda e: e.dma_start(out=Kr[:, t0:t0 + G], in_=s_kr[:]), reads=[B_kr], dma="st_kr")
                    for c in range(4):
                        ap, b = lin(O_POOL + c * 128, s)
                        evac_copy(s_pl[:, c, :], ap, b, [B_pl])
                    p.add("sp", lambda e: e.dma_start(out=pin[:, :, t0:t0 + G].rearrange("c p t -> p c t"), in_=s_pl[:]),
                          reads=[B_pl], dma="st_pl")
                    for gb3 in range(3):
                        gs = gb3 % 2
                        for c in range(8):
                            j = gb3 * 8 + c
                            ap, b = lin(O_GATE + j * 128, s)
                            p.add("act", lambda e, ap=ap, gs=gs, c=c, j=j: e.activation(
                                out=s_gt[gs][:, c, :], in_=ap, func=AF.Sigmoid, bias=scs("bgate", li * 24 + j), scale=1.0),
                                reads=b + [Bsc], writes=[B_gt[gs]])
                        p.add("sp", lambda e, gs=gs, gb3=gb3: e.dma_start(
                            out=gat[gb3 * 8:(gb3 + 1) * 8, :, t0:t0 + G].rearrange("c p t -> p c t"), in_=s_gt[gs][:]),
                            reads=[B_gt[gs]], dma=f"st_gt{gs}")
                    for sub in range(2):
                        fb = frot.next()
                        fap, fbufs = psf(fb)
                        for k in range(8):
                            p.add("pe", lambda e, fap=fap, k=k, sub=sub: e.matmul(
                                fap, u[s][:, k, sub * 128:(sub + 1) * 128], wi[:, k, O_DV:O_DV + 512], start=(k == 0), stop=(k == 7)),
                                reads=[Bw, Bu[s]], writes=fbufs)
                        evac_copy(s_dv[:, sub, :, 0:64], ps[fb][:, :].rearrange("p (h d) -> p h d", d=64), fbufs, [B_dv])
                    p.add("sp", lambda e: e.dma_start(out=Vd[t0:t0 + G, :, :].rearrange("(s p) h d -> p s h d", p=128), in_=s_dv[:]),
                          reads=[B_dv], dma="st_dv")
                    for c in range(4):
                        ap, b = lin(c * 128, s, wt=wq, kin=3, rhs=cqn, Brhs=Bcqn)
                        evac_copy(s_qm[:, c, :], ap, b, [B_qm])
                    for c in range(2):
                        hp = 2 * brot.next()
                        apA, bA = lin(512 + c * 128, s, wt=wq, kin=3, rhs=cqn, Brhs=Bcqn, hp=hp)
                        apB, bB = lin(768 + c * 128, s, wt=wq, kin=3, rhs=cqn, Brhs=Bcqn, hp=hp + 1)
                        rope_out(apA, bA, apB, bB, s_qm[:, 4 + c, :], B_qm, s)
                    p.add("sp", lambda e: e.dma_start(out=Qm[:, t0:t0 + G].rearrange("(c p) t -> p c t", p=128), in_=s_qm[:]),
                          reads=[B_qm], dma="st_qm")
                    for c in range(4):
                        ap, b = lin(c * 128, s, wt=wkv, kin=2, rhs=ckvn, Brhs=Bckvn)
                        evac_copy(s_kn[:, c, :], ap, b, [B_kn])
                    p.add("sp", lambda e: e.dma_start(out=Kmn[:, t0:t0 + G].rearrange("(c p) t -> p c t", p=128), in_=s_kn[:]),
                          reads=[B_kn], dma="st_kn")
                    for sub in range(2):
                        fb = frot.next()
                        fap, fbufs = psf(fb)
                        for k in range(2):
                            p.add("pe", lambda e, fap=fap, k=k, sub=sub: e.matmul(
                                fap, ckvn[:, k, sub * 128:(sub + 1) * 128], wkv[:, k, 512:1024], start=(k == 0), stop=(k == 1)),
                                reads=[Bw, Bckvn], writes=fbufs)
                        evac_copy(s_v[:, sub, :, 0:64], ps[fb][:, :].rearrange("p (h d) -> p h d", d=64), fbufs, [B_v])
                    p.add("sp", lambda e: e.dma_start(out=Vm[t0:t0 + G, :, :].rearrange("(s p) h d -> p s h d", p=128), in_=s_v[:]),
                          reads=[B_v], dma="st_v")

                load_x(0)
                norm(0)
                for g in range(NG):
                    if g + 1 < NG:
                        load_x(g + 1)
                    proj(g)
                    if g + 1 < NG:
                        norm(g + 1)
                p.flush()

        def phase_att(li):
            with ExitStack() as st:
                Kt = [sb(st, [96, T], BF16, "Kt") for _ in range(2)]
                Qt = [sb(st, [96, T], BF16, "Qt") for _ in range(2)]
                Qt_main = Qt
                Qz = [sb(st, [96, T], BF16, "Qz") for _ in range(2)]
                Vt = [sb(st, [128, NKC, 65], BF16, "Vt") for _ in range(2)]
                Bop = [Buf(), Buf()]
                pt = Rot([(sb(st, [128, 512], BF16, "pt"), Buf()) for _ in range(4)])
                oa = [sb(st, [65, 512], F32, "oa") for _ in range(2)]
                Boa = [Buf(), Buf()]
                rr = [sb(st, [65, 512], F32, "rr") for _ in range(2)]
                Brr = [Buf(), Buf()]
                t1 = sb(st, [64, 512], F32, "t1"); Bt1 = Buf()
                t2 = sb(st, [64, 512], F32, "t2"); Bt2 = Buf()
                sqd = sb(st, [64, 512], BF16, "sqd"); Bsqd = Buf()
                rs = sb(st, [64, 512], F32, "rs"); Brs = Buf()
                ost = Rot([(sb(st, [64, 512], BF16, "ost"), Buf(), f"sto{i}") for i in range(2)])
                srot = Rot([0, 1, 2])
                orot = Rot([3, 4, 5])
                xrot = Rot([6, 7])
                pending = []

                def run_step():
                    if pending:
                        pending.pop(0)()

                def install(steps):
                    while pending:
                        pending.pop(0)()
                    pending.extend(steps)
                qgroups = [(0, CTX, 0, CTX // 128)] + [(CTX + q * 512, 512, 0, NKC) for q in range(NQG)]
                hs = [1 if "hs1" in DBG else 0]

                def attend(s, qrows, krows, q0, N, kc0, kc1, scale, Qt=None):
                    Qt = Qt_main if Qt is None else Qt
                    ob = orot.next()
                    oap, obufs = psf(ob)
                    LA = 2
                    sb_list = []

                    def s_mm(kc):
                        b = srot.next()
                        sap, sbufs = psf(b)
                        p.add("pe", lambda e: e.matmul(ps[b][:, 0:N], Kt[s][krows[0]:krows[1], kc * 128:(kc + 1) * 128],
                                                       Qt[s][qrows[0]:qrows[1], q0:q0 + N], start=True, stop=True),
                              reads=[Bop[s]], writes=sbufs)
                        sb_list.append((b, sbufs))

                    kcs = list(range(kc0, kc1))
                    for i in range(min(LA, len(kcs))):
                        s_mm(kcs[i])
                    for i, kc in enumerate(kcs):
                        if i + LA < len(kcs):
                            s_mm(kcs[i + LA])
                        b, sbufs = sb_list[i]
                        p_t, p_b = pt.next()
                        p.add("act", lambda e, b=b, p_t=p_t: e.activation(out=p_t[:, 0:N], in_=ps[b][:, 0:N], func=AF.Exp, scale=scale),
                              reads=sbufs, writes=[p_b])
                        p.add("pe", lambda e, p_t=p_t, kc=kc, i=i: e.matmul(ps[ob][0:65, 0:N], Vt[s][:, kc, :], p_t[:, 0:N],
                                                                           start=(i == 0), stop=(i == len(kcs) - 1)),
                              reads=[p_b, Bop[s]], writes=obufs)
                        if i in (1, 4, 8):
                            run_step()
                    return ob, obufs

                def fin_a(ob, obufs, N, w):
                    p.add("act", lambda e: e.activation(out=oa[w][:, 0:N], in_=ps[ob][0:65, 0:N], func=AF.Identity),
                          reads=obufs, writes=[Boa[w]])
                    p.add("dve", lambda e: e.reciprocal(out=rr[w][64:65, 0:N], in_=oa[w][64:65, 0:N]), reads=[Boa[w]], writes=[Brr[w]])

                def fin_b(N, w):
                    xb = xrot.next()
                    xap, xbufs = psf(xb)
                    p.add("pe", lambda e: e.matmul(ps[xb][0:64, 0:N], ones_f[64:65, 0:64], rr[w][64:65, 0:N], start=True, stop=True),
                          reads=[Brr[w], Bones], writes=xbufs)
                    return xb, xbufs

                def mla_steps(ob, obufs, N, h, q0):
                    def step_a():
                        fin_a(ob, obufs, N, 0)

                    def step_b():
                        xb, xbufs = fin_b(N, 0)
                        o_t, o_b, o_k = ost.next()
                        p.add("dve", lambda e: e.tensor_tensor(out=o_t[:, 0:N], in0=oa[0][0:64, 0:N], in1=ps[xb][0:64, 0:N], op=ALU.mult),
                              reads=[Boa[0]] + xbufs, writes=[o_b])
                        p.add("sp", lambda e: e.dma_start(
                            out=omla[h // 2, (h % 2) * 64:(h % 2) * 64 + 64, q0:q0 + N], in_=o_t[:, 0:N]), reads=[o_b], dma=o_k)
                    return [step_a, step_b]

                def diff_steps(ob1, obufs1, ob2, obufs2, N, h, q0):
                    def step_a():
                        fin_a(ob1, obufs1, N, 0)
                        fin_a(ob2, obufs2, N, 1)

                    def step_b():
                        xb1, xbufs1 = fin_b(N, 0)
                        xb2, xbufs2 = fin_b(N, 1)
                        p.add("dve", lambda e: e.tensor_tensor(out=t1[:, 0:N], in0=oa[0][0:64, 0:N], in1=ps[xb1][0:64, 0:N], op=ALU.mult),
                              reads=[Boa[0]] + xbufs1, writes=[Bt1])
                        p.add("dve", lambda e: e.tensor_tensor(out=t2[:, 0:N], in0=oa[1][0:64, 0:N], in1=ps[xb2][0:64, 0:N], op=ALU.mult),
                              reads=[Boa[1]] + xbufs2, writes=[Bt2])
                        p.add("dve", lambda e: e.scalar_tensor_tensor(out=t1[:, 0:N], in0=t2[:, 0:N], scalar=lamt[0:64, li, 0:1], in1=t1[:, 0:N],
                                                                    op0=ALU.mult, op1=ALU.add), reads=[Bt1, Bt2, Bmod], writes=[Bt1])
                        p.add("pool", lambda e: e.tensor_tensor(out=sqd[:, 0:N], in0=t1[:, 0:N], in1=t1[:, 0:N], op=ALU.mult),
                              reads=[Bt1], writes=[Bsqd])

                    def step_c():
                        xb = xrot.next()
                        xap, xbufs = psf(xb)
                        p.add("pe", lambda e: e.matmul(ps[xb][0:64, 0:N], ones_bf[0:64, 0:64], sqd[:, 0:N], start=True, stop=True),
                              reads=[Bsqd, Bones], writes=xbufs)
                        p.add("act", lambda e: e.activation(out=rs[:, 0:N], in_=ps[xb][0:64, 0:N], func=AF.Sqrt, scale=1.0 / 64.0, bias=epsb[0:64, 0:1]),
                              reads=xbufs + [Bones], writes=[Brs])
                        p.add("dve", lambda e: e.reciprocal(out=rs[:, 0:N], in_=rs[:, 0:N]), reads=[Brs], writes=[Brs])
                        o_t, o_b, o_k = ost.next()
                        p.add("dve", lambda e: e.scalar_tensor_tensor(out=o_t[:, 0:N], in0=t1[:, 0:N], scalar=lamt[0:64, li, 1:2], in1=rs[:, 0:N],
                                                                    op0=ALU.mult, op1=ALU.mult), reads=[Bt1, Brs, Bmod], writes=[o_b])
                        p.add("sp", lambda e: e.dma_start(
                            out=odiff[h // 2, (h % 2) * 64:(h % 2) * 64 + 64, q0:q0 + N], in_=o_t[:, 0:N]), reads=[o_b], dma=o_k)
                    return [step_a, step_b, step_c]

                for h in range(NH):
                    if "evenonly" in DBG and h % 2 == 1:
                        continue
                    if "mla1" in DBG and h > 0:
                        continue
                    if "nomla" in DBG:
                        continue
                    s = hs[0] % 2
                    hs[0] += 1
                    for (dst, src) in ((Kt[s][0:64, :], Kmn[h * 64:(h + 1) * 64, :]), (Kt[s][64:96, :], Kr[:, :]),
                                       (Qt[s][0:64, :], Qm[h * 64:(h + 1) * 64, :]), (Qt[s][64:96, :], Qm[512 + h * 32:512 + (h + 1) * 32, :])):
                        p.add("sp", lambda e, dst=dst, src=src: e.dma_start(out=dst, in_=src), writes=[Bop[s]], dma=f"op{s}")
                    for c0 in range(0, NKC, 11):
                        c1 = min(NKC, c0 + 11)
                        p.add("sp", lambda e, h=h, s=s, c0=c0, c1=c1: e.dma_start(
                            out=Vt[s][:, c0:c1, :], in_=Vm[c0 * 128:c1 * 128, h, :].rearrange("(c p) d -> p c d", p=128)),
                            writes=[Bop[s]], dma=f"op{s}")
                    for (q0, N, kc0, kc1) in qgroups:
                        ob, obufs = attend(s, (0, 96), (0, 96), q0, N, kc0, kc1, MLA_SCALE)
                        install(mla_steps(ob, obufs, N, h, q0))

                for s in range(2):
                    p.add("dve", lambda e, s=s: e.memset(Qt[s][32:64, :], 0.0), writes=[Bop[s]])
                    p.add("dve", lambda e, s=s: e.memset(Qt[s][64:96, :], 0.0), writes=[Bop[s]])
                    p.add("dve", lambda e, s=s: e.memset(Qz[s][0:32, :], 0.0), writes=[Bop[s]])
                    p.add("dve", lambda e, s=s: e.memset(Qz[s][64:96, :], 0.0), writes=[Bop[s]])
                for h in range(NH):
                    if "evenonly" in DBG or "nodiff" in DBG:
                        continue
                    if "diff1" in DBG and h > 0:
                        continue
                    s = hs[0] % 2
                    hs[0] += 1
                    for (dst, src) in ((Kt[s][0:32, :], Kd[h * 64:h * 64 + 32, :]), (Kt[s][32:64, :], Kd[h * 64 + 32:h * 64 + 64, :]),
                                       (Qt[s][0:32, :], Qd[h * 64:h * 64 + 32, :]), (Qz[s][32:64, :], Qd[h * 64 + 32:h * 64 + 64, :])):
                        p.add("sp", lambda e, dst=dst, src=src: e.dma_start(out=dst, in_=src), writes=[Bop[s]], dma=f"op{s}")
                    for c0 in range(0, NKC, 11):
                        c1 = min(NKC, c0 + 11)
                        p.add("sp", lambda e, h=h, s=s, c0=c0, c1=c1: e.dma_start(
                            out=Vt[s][:, c0:c1, :], in_=Vd[c0 * 128:c1 * 128, h, :].rearrange("(c p) d -> p c d", p=128)),
                            writes=[Bop[s]], dma=f"op{s}")
                    for (q0, N, kc0, kc1) in qgroups:
                        ob1, obufs1 = attend(s, (0, 96), (0, 96), q0, N, kc0, kc1, DIFF_SCALE, Qt=Qt)
                        ob2, obufs2 = attend(s, (0, 96), (0, 96), q0, N, kc0, kc1, DIFF_SCALE, Qt=Qz)
                        if "nofin" in DBG:
                            continue
                        install(diff_steps(ob1, obufs1, ob2, obufs2, N, h, q0))
                install([])
                p.flush()

        def phase_pool(li):
            PW = 512
            HALO = 8
            with ExitStack() as st:
                pp = sb(st, [128, 4, 128], BF16, "pp")
                Bw = Buf()
                p.add("pool", lambda e: e.dma_start(out=pp[:], in_=pool_proj[li].rearrange("g c d -> c g d")), writes=[Bw], dma="wload", nodep=True)
                xin = [sb(st, [128, PW + 2 * HALO], F32, "pxin") for _ in range(2)]
                Bxin = [Buf(), Buf()]
                rcs = [sb(st, [128, PW], F32, "prc") for _ in range(2)]
                sa = sb(st, [128, PW + 2 * HALO], F32, "psa"); sbb = sb(st, [128, PW + 2 * HALO], F32, "psb")
                Bsa = Buf(); Bsb = Buf()
                pl = Rot([(sb(st, [128, PW], BF16, "ppl"), Buf()) for _ in range(2)])
                ost = Rot([(sb(st, [128, PW], BF16, "pos"), Buf(), f"psto{i}") for i in range(2)])
                brot = Rot([0, 1, 2, 3])
                blocks = [(0, CTX, 0, CTX)] + [(CTX, T, CTX + q * PW, PW) for q in range(SEQ // PW)]
                it = 0
                for gi, w in enumerate(POOL_WINDOWS):
                    for (s0, s1, b0, W) in blocks:
                        s = it % 2
                        it += 1
                        lo = max(s0, b0 - HALO)
                        hi = min(s1, b0 + W + HALO)
                        X = xin[s]
                        p.add("dve", lambda e, X=X: e.memset(X[:], 0.0), writes=[Bxin[s]])
                        p.add("sp", lambda e, X=X, gi=gi, lo=lo, hi=hi, b0=b0: e.dma_start(
                            out=X[:, lo - (b0 - HALO):hi - (b0 - HALO)], in_=pin[gi, :, lo:hi]), writes=[Bxin[s]], dma=f"pl{s}")
                        p.add("sp", lambda e, s=s, gi=gi, b0=b0, W=W: e.dma_start(out=rcs[s][:, 0:W], in_=rcT[:, gi, b0:b0 + W]),
                              writes=[Bxin[s]], dma=f"pl{s}")
                        E = PW + 2 * HALO
                        WW = W + 2 * HALO
                        p.add("dve", lambda e, X=X, WW=WW: e.tensor_tensor(out=sa[:, 1:WW], in0=X[:, 0:WW - 1], in1=X[:, 1:WW], op=ALU.add),
                              reads=[Bxin[s]], writes=[Bsa])
                        cur, Bcur, oth, Both = sa, Bsa, sbb, Bsb
                        vlo, vhi = 1, WW
                        sh = 1
                        ww = 2
                        while ww < w:
                            nlo, nhi = vlo + sh, vhi - sh
                            p.add("dve", lambda e, cur=cur, oth=oth, nlo=nlo, nhi=nhi, sh=sh: e.tensor_tensor(
                                out=oth[:, nlo:nhi], in0=cur[:, nlo - sh:nhi - sh], in1=cur[:, nlo + sh:nhi + sh], op=ALU.add),
                                reads=[Bcur], writes=[Both])
                            cur, Bcur, oth, Both = oth, Both, cur, Bcur
                            vlo, vhi = nlo, nhi
                            sh *= 2
                            ww *= 2
                        assert vlo <= HALO and vhi >= HALO + W, (vlo, vhi, w)
                        p.add("dve", lambda e, cur=cur, oth=oth, s=s, W=W: e.tensor_tensor(
                            out=oth[:, HALO:HALO + W], in0=cur[:, HALO:HALO + W], in1=rcs[s][:, 0:W], op=ALU.mult),
                            reads=[Bcur, Bxin[s]], writes=[Both])
                        pl_t, pl_b = pl.next()
                        p.add("dve", lambda e, oth=oth, X=X, pl_t=pl_t, W=W: e.tensor_tensor(
                            out=pl_t[:, 0:W], in0=oth[:, HALO:HALO + W], in1=X[:, HALO:HALO + W], op=ALU.subtract),
                            reads=[Both, Bxin[s]], writes=[pl_b])
                        bk = brot.next()
                        bap, bbufs = psf(bk)
                        p.add("pe", lambda e, bk=bk, gi=gi, pl_t=pl_t, W=W: e.matmul(ps[bk][:, 0:W], pp[:, gi, :], pl_t[:, 0:W], start=True, stop=True),
                              reads=[Bw, pl_b], writes=bbufs)
                        o_t, o_b, o_k = ost.next()
                        p.add("dve", lambda e, bk=bk, o_t=o_t, gi=gi, W=W: e.tensor_scalar(
                            out=o_t[:, 0:W], in0=ps[bk][:, 0:W], scalar1=scs("poolb", li * 4 + gi), scalar2=scs("pools", li * 4 + gi),
                            op0=ALU.add, op1=ALU.mult), reads=bbufs + [Bsc], writes=[o_b])
                        p.add("sp", lambda e, o_t=o_t, gi=gi, b0=b0, W=W: e.dma_start(out=opool[gi, :, b0:b0 + W], in_=o_t[:, 0:W]),
                              reads=[o_b], dma=o_k)
                p.flush()

        def phase_merge(li):
            with ExitStack() as st:
                wb = [sb(st, [128, 4, D], BF16, "wbr") for _ in range(3)]
                wo = sb(st, [128, 8, D], BF16, "wo")
                Bw = Buf()
                for i in range(3):
                    load_weights_cast(lambda k, i=i: wb[i][:, k, :], lambda k, i=i: wbr[i][li, k * 128:(k + 1) * 128, :], 4, Bw)
                load_weights_cast(lambda k: wo[:, k, :], lambda k: w_out[li, k * 128:(k + 1) * 128, :], 8, Bw)
                xg = [sb(st, [128, 8, G], F32, "xg") for _ in range(2)]
                Bx = [Buf(), Buf()]
                ob = [[sb(st, [128, 4, G], BF16, "obr") for _ in range(3)] for _ in range(2)]
                Bob = [Buf(), Buf()]
                gt = [sb(st, [128, 24, G], F32, "gt") for _ in range(2)]
                Bgt = [Buf(), Buf()]
                mg = sb(st, [128, 8, G], BF16, "mg")
                Bmg = Buf()
                mt = Rot([((sb(st, [128, G], F32, "m1"), sb(st, [128, G], F32, "m2"), sb(st, [128, G], F32, "m3")), Buf()) for _ in range(2)])
                prot3 = Rot([(0, 1, 2), (4, 5, 6)])
                yrot = Rot([8, 10, 12, 14])
                srcs = (omla, odiff, opool)

                def load(g):
                    s = g % 2
                    t0 = g * G
                    p.add("sp", lambda e: e.dma_start(out=xg[s][:], in_=xres[:, :, t0:t0 + G].rearrange("c p t -> p c t")),
                          writes=[Bx[s]], dma=f"xl{s}")
                    for i in range(3):
                        p.add("sp", lambda e, i=i: e.dma_start(out=ob[s][i][:], in_=srcs[i][:, :, t0:t0 + G].rearrange("c p t -> p c t")),
                              writes=[Bob[s]], dma=f"obl{s}")
                    p.add("sp", lambda e: e.dma_start(out=gt[s][:], in_=gat[:, :, t0:t0 + G].rearrange("c p t -> p c t")),
                          writes=[Bgt[s]], dma=f"gtl{s}")

                def merge(g):
                    s = g % 2
                    jx = mod_cols(g)
                    t0 = g * G
                    for c in range(8):
                        aps = []
                        hps = prot3.next()
                        for i in range(3):
                            ap, b = psh(hps[i])
                            for k in range(4):
                                p.add("pe", lambda e, ap=ap, i=i, k=k, c=c: e.matmul(ap, wb[i][:, k, c * 128:(c + 1) * 128], ob[s][i][:, k, :],
                                                                                   start=(k == 0), stop=(k == 3)),
                                      reads=[Bw, Bob[s]], writes=b)
                            aps.append((ap, b))
                        (m1, m2, m3), mb = mt.next()
                        for i, m in enumerate((m1, m2, m3)):
                            p.add("dve", lambda e, i=i, m=m, c=c, ap=aps[i][0]: e.tensor_tensor(out=m[:], in0=ap, in1=gt[s][:, i * 8 + c, :], op=ALU.mult),
                                  reads=aps[i][1] + [Bgt[s]], writes=[mb])
                        p.add("pool", lambda e, m1=m1, m2=m2: e.tensor_tensor(out=m1[:], in0=m1[:], in1=m2[:], op=ALU.add), reads=[mb], writes=[mb])
                        p.add("pool", lambda e, m1=m1, m3=m3, c=c: e.tensor_tensor(out=mg[:, c, :], in0=m1[:], in1=m3[:], op=ALU.add),
                              reads=[mb], writes=[Bmg])
                    for c in range(8):
                        yap, yb = psh(yrot.next())
                        for k in range(8):
                            p.add("pe", lambda e, yap=yap, k=k, c=c: e.matmul(yap, wo[:, k, c * 128:(c + 1) * 128], mg[:, k, :], start=(k == 0), stop=(k == 7)),
                                  reads=[Bw, Bmg], writes=yb)
                        p.add("dve", lambda e, yap=yap, c=c: e.scalar_tensor_tensor(
                            out=xg[s][:, c, :], in0=yap, scalar=modf[:, li, 5 * 8 + c, jx:jx + 1], in1=xg[s][:, c, :], op0=ALU.mult, op1=ALU.add),
                            reads=yb + [Bx[s], Bmod], writes=[Bx[s]])
                    p.add("sp", lambda e: e.dma_start(out=xres[:, :, t0:t0 + G].rearrange("c p t -> p c t"), in_=xg[s][:]),
                          reads=[Bx[s]], dma=f"xs{s}")

                load(0)
                for g in range(NG):
                    if g + 1 < NG:
                        load(g + 1)
                    merge(g)
                p.flush()

        class _Stop(Exception):
            pass

        def chk(name):
            if stop_after == name:
                raise _Stop()
        try:
            chk("M")
            for li in range(L):
                last = (li == L - 1)
                if "attonly" not in DBG:
                    phase_ffn(li, "a", 0, xT if li == 0 else xres, xres, False)
                chk("ffa")
                if "onlyffn" not in DBG:
                    if "attonly" not in DBG:
                        phase_proj(li)
                    chk("proj")
                    phase_att(li)
                    chk("att")
                    phase_pool(li)
                    chk("pool")
                    phase_merge(li)
                    chk("merge")
                phase_ffn(li, "b", 2, xres, outT if last else xres, final and last)
        except _Stop:
            dbg = top.enter_context(nc.sbuf_tensor("dbgmod", [128, L * 144 + L * 48 * 2 + L * 4], F32))
            Bd = Buf()
            p.add("dve", lambda e: e.tensor_copy(out=dbg[:, 0:L * 144], in_=modf[:].rearrange("p l j x -> p (l j x)")), reads=[Bmod], writes=[Bd])
            p.add("dve", lambda e: e.tensor_copy(out=dbg[:, L * 144:L * 192], in_=modA[:].rearrange("p l i c x -> p (l i c x)")), reads=[Bmod], writes=[Bd])
            p.add("dve", lambda e: e.tensor_copy(out=dbg[:, L * 192:L * 240], in_=modG[:].rearrange("p l i c x -> p (l i c x)")), reads=[Bmod], writes=[Bd])
            p.add("dve", lambda e: e.tensor_copy(out=dbg[:, L * 240:L * 244], in_=lamt[:].rearrange("p l x -> p (l x)")), reads=[Bmod], writes=[Bd])
            p.add("sp", lambda e: e.dma_start(out=outT[0, :, 0:L * 244], in_=dbg[:]), reads=[Bd], dma="misc")
        p.add("dve", lambda e: e.memset(epsb[:], EPS), writes=[Bones])
        p.flush(final=True)
        nops = p.nops
    return nc, nops


def _swap_half32(a):
    sh = a.shape
    b = a.reshape(*sh[:-1], sh[-1] // 32, 2, 16)
    return np.ascontiguousarray(b[..., ::-1, :]).reshape(sh)


def _rope_tables(SEQ):
    rows = SEQ // 64
    row_ids = np.repeat(np.arange(rows), 64).astype(np.float32)
    col_ids = np.tile(np.arange(64), rows).astype(np.float32)
    inv = (np.float32(10000.0) ** (-np.arange(8, dtype=np.float32) / np.float32(8))).astype(np.float32)
    ang = np.concatenate([row_ids[:, None] * inv, col_ids[:, None] * inv], axis=-1).astype(np.float32)
    c = np.cos(ang).astype(np.float32).T
    s = np.sin(ang).astype(np.float32).T
    T = CTX + SEQ
    cosT = np.ones((128, T), np.float32)
    sinT = np.zeros((128, T), np.float32)
    for pp in range(128):
        i = pp % 32
        cosT[pp, CTX:] = c[i % 16]
        sinT[pp, CTX:] = -s[i % 16] if i < 16 else s[i % 16]
    return cosT, sinT


def _pool_rc(SEQ):
    T = CTX + SEQ
    rc = np.zeros((4, T), np.float32)
    for gi, w in enumerate(POOL_WINDOWS):
        for (s0, n) in ((0, CTX), (CTX, SEQ)):
            t = np.arange(n)
            lo = np.clip(t - w // 2, 0, n)
            hi = np.clip(t + w - w // 2, 0, n)
            rc[gi, s0:s0 + n] = (1.0 / (hi - lo).astype(np.float32)).astype(np.float32)
    return np.ascontiguousarray(np.broadcast_to(rc[None], (128, 4, T)))


def _pp(v):
    v = np.asarray(v, np.float32).reshape(-1, 128)
    return np.ascontiguousarray(v.T)


def prep_shared(inp, layer_ids, SEQ):
    L = len(layer_ids)
    ls = list(layer_ids)
    f = lambda k: np.asarray(inp[k], np.float32)
    w_in = f("w_in")[ls]
    cq, ckv, kr = w_in[..., 0:384], w_in[..., 384:640], w_in[..., 640:672]
    dq, dk, dv = w_in[..., 672:1184], w_in[..., 1184:1696], w_in[..., 1696:2208]
    pl, gt = w_in[..., 2208:2720], w_in[..., 2720:5792]
    w_in_ext = np.concatenate([cq, ckv, dq, _swap_half32(dq), dk, _swap_half32(dk), pl, gt, dv, kr, _swap_half32(kr)], axis=-1)
    assert w_in_ext.shape[-1] == WIN
    wuq = f("mla_w_uq")[ls].reshape(L, 384, NH, 96)
    nope = wuq[..., :64].reshape(L, 384, 512)
    rope = wuq[..., 64:].reshape(L, 384, 256)
    w_uq_ext = np.concatenate([nope, rope, _swap_half32(rope)], axis=-1)
    wukv = f("mla_w_ukv")[ls].reshape(L, 256, NH, 128)
    w_ukv_r = np.concatenate([wukv[..., :64].reshape(L, 256, 512), wukv[..., 64:].reshape(L, 256, 512)], axis=-1)
    cosT, sinT = _rope_tables(SEQ)
    d = {
        "ada_w": np.ascontiguousarray(f("ada_w")[ls]),
        "ffa_wg": np.ascontiguousarray(f("ffa_w_gate")[ls]), "ffa_wu": np.ascontiguousarray(f("ffa_w_up")[ls]),
        "ffa_wd": np.ascontiguousarray(f("ffa_w_down")[ls]),
        "ffb_wg": np.ascontiguousarray(f("ffb_w_gate")[ls]), "ffb_wu": np.ascontiguousarray(f("ffb_w_up")[ls]),
        "ffb_wd": np.ascontiguousarray(f("ffb_w_down")[ls]),
        "w_in_ext": np.ascontiguousarray(w_in_ext), "w_uq_ext": np.ascontiguousarray(w_uq_ext),
        "w_ukv_r": np.ascontiguousarray(w_ukv_r),
        "pool_proj": np.ascontiguousarray(f("pool_proj")[ls]),
        "w_br_mla": np.ascontiguousarray(f("w_br_mla")[ls]), "w_br_diff": np.ascontiguousarray(f("w_br_diff")[ls]),
        "w_br_pool": np.ascontiguousarray(f("w_br_pool")[ls]), "w_out": np.ascontiguousarray(f("w_out")[ls]),
        "cosT": cosT, "sinT": sinT, "rcT": _pool_rc(SEQ),
    }
    SOFF, NS = small_layout(L)
    sm = np.zeros((128, NS), np.float32)

    def put(name, arr):
        arr = np.asarray(arr, np.float32)
        sm[:, SOFF[name]:SOFF[name] + arr.shape[1]] = arr
    put("adab", np.concatenate([_pp(f("ada_b")[l]) for l in ls], axis=1))
    put("normg", np.concatenate([_pp(f("norm_g")[l].reshape(-1)) for l in ls], axis=1))
    put("bgate", np.concatenate([_pp(f("b_gate")[l].reshape(-1)) for l in ls], axis=1))
    put("gq", np.concatenate([_pp(f("mla_q_norm_g")[l]) for l in ls], axis=1))
    put("gkv", np.concatenate([_pp(f("mla_kv_norm_g")[l]) for l in ls], axis=1))
    put("dlam", np.concatenate([np.broadcast_to(f("diff_lambda")[l].reshape(1, 128), (128, 128)) for l in ls], axis=1))
    put("subg", np.stack([np.tile(f("diff_subln_g")[l], 2) for l in ls], axis=1))
    put("poolb", np.concatenate([_pp(f("pool_b")[l].reshape(-1)) for l in ls], axis=1))
    put("pools", np.concatenate([_pp(f("pool_scale")[l]) for l in ls], axis=1))
    put("fg", _pp(f("final_g")))
    return d, sm


def core_inputs(shared, sm, c_b, c_ctx, xT, L):
    SOFF, NS = small_layout(L)
    sm = sm.copy()
    cc = np.stack([_pp(c_b), _pp(c_ctx)], axis=-1).reshape(128, 16)
    sm[:, SOFF["cc"]:SOFF["cc"] + 16] = cc
    d = dict(shared)
    d["smallc"] = sm
    d["xT"] = xT
    return d


def to_fm(tok):
    return np.ascontiguousarray(np.asarray(tok, np.float32).T.reshape(KC, 128, -1))


_PROG_CACHE = {}


def get_prog(SEQ, layers, final):
    key = (SEQ, tuple(layers), final)
    if key not in _PROG_CACHE:
        _PROG_CACHE[key] = build_program(SEQ, list(layers), final)
    return _PROG_CACHE[key][0]


N_CORES = 4


def kernel(**inp):
    x = np.asarray(inp["x"], np.float32)
    ctx = np.asarray(inp["ctx"], np.float32)
    c = np.asarray(inp["c"], np.float32)
    c_ctx = np.asarray(inp["c_ctx"], np.float32)
    B, SEQ, _ = x.shape
    DEPTH = inp["ada_w"].shape[0]
    layers = list(range(DEPTH))
    shared, sm = prep_shared(inp, layers, SEQ)
    nc = get_prog(SEQ, layers, True)
    in_maps = []
    for b in range(B):
        xT = to_fm(np.concatenate([ctx[b], x[b]], axis=0))
        in_maps.append(core_inputs(shared, sm, c[b], c_ctx, xT, DEPTH))
    res = run_bass_kernel_spmd(nc, in_maps, core_ids=list(range(B)))
    out = np.empty((B, SEQ, D), np.float32)
    for b in range(B):
        o = np.asarray(res.results[b]["outT"], np.float32)
        out[b] = o.reshape(D, SEQ).T
    return out
```
